# Optimizing a Trainium2 kernel written in Bass

```python
import math
import jax, jax.numpy as jnp
from jax import lax
import numpy as np

D_MODEL = 1024
BATCH = 8
SEQ = 4096
DEPTH = 2

FOX_WIDTH = D_MODEL // 2
FOX_HEADS = 8
FOX_HEAD_DIM = FOX_WIDTH // FOX_HEADS
S5_WIDTH = D_MODEL - FOX_WIDTH
S5_GROUP = 16
S5_GROUPS = S5_WIDTH // S5_GROUP
S5_STATE = 64
Q_BLOCK = 128
EVEN_IN = 3 * FOX_WIDTH + FOX_HEADS + S5_WIDTH
CONV_WIDTH = D_MODEL // 2
CONV_K = 3
RET_WIDTH = D_MODEL - CONV_WIDTH
RET_HEADS = 4
RET_HEAD_DIM = RET_WIDTH // RET_HEADS
RET_CHUNK = 128
ODD_IN = 3 * CONV_WIDTH + 4 * RET_WIDTH
D_FF = 4 * D_MODEL
EPS = 1e-6
N_EVEN = (DEPTH + 1) // 2
N_ODD = DEPTH // 2

kernel_name = "hybrid_fox_s5_shortconv_retention"

F32 = jnp.float32


def rms_norm(x, g):
    xf = x.astype(F32)
    y = xf * lax.rsqrt(jnp.mean(xf * xf, axis=-1, keepdims=True) + EPS)
    return (y * g.astype(F32)).astype(x.dtype)


def forgetting_attention(q, k, v, log_f):
    b, l, h, dh = q.shape
    nblk = l // Q_BLOCK
    c = jnp.cumsum(log_f.astype(F32), axis=1).transpose(0, 2, 1)
    scale = dh ** -0.5
    q_blocks = q.reshape(b, nblk, Q_BLOCK, h, dh).transpose(1, 0, 3, 2, 4)
    c_blocks = c.reshape(b, h, nblk, Q_BLOCK).transpose(2, 0, 1, 3)
    k_pos = jnp.arange(l)

    def block(args):
        i, q_i, cq_i = args
        s = jnp.einsum('bhqd,bkhd->bhqk', q_i, k).astype(F32) * scale
        s = s + cq_i[..., None] - c[:, :, None, :]
        q_pos = i * Q_BLOCK + jnp.arange(Q_BLOCK)
        causal = k_pos[None, :] <= q_pos[:, None]
        s = jnp.where(causal, s, -jnp.inf)
        p = jax.nn.softmax(s, axis=-1).astype(v.dtype)
        return jnp.einsum('bhqk,bkhd->bqhd', p, v)

    out = lax.map(block, (jnp.arange(nblk), q_blocks, c_blocks))
    return out.transpose(1, 0, 2, 3, 4).reshape(b, l, h * dh)


def s5_layer(u, log_dt, lam_re, lam_im, b_re, b_im, c_re, c_im, d_skip, w_glu, b_glu):
    b, l, _ = u.shape
    uf = u.astype(F32).reshape(b, l, S5_GROUPS, S5_GROUP)
    lam = lax.complex(lam_re.astype(F32), lam_im.astype(F32))
    dt = jnp.exp(log_dt.astype(F32))[:, None]
    lam_bar = jnp.exp(lam * dt)
    b_mat = lax.complex(b_re.astype(F32), b_im.astype(F32))
    b_bar = ((lam_bar - 1.0) / lam)[..., None] * b_mat
    c_mat = lax.complex(c_re.astype(F32), c_im.astype(F32))
    bu = jnp.einsum('blgh,gph->blgp', uf, b_bar)
    a = jnp.broadcast_to(lam_bar, (1, l, S5_GROUPS, S5_STATE))

    def combine(e1, e2):
        a1, b1 = e1
        a2, b2 = e2
        return a1 * a2, a2 * b1 + b2

    _, states = lax.associative_scan(combine, (a, bu), axis=1)
    y = jnp.real(jnp.einsum('blgp,ghp->blgh', states, c_mat))
    y = y + d_skip.astype(F32).reshape(S5_GROUPS, S5_GROUP) * uf
    y = jax.nn.gelu(y.reshape(b, l, S5_WIDTH))
    y = y * jax.nn.sigmoid(y @ w_glu.astype(F32) + b_glu.astype(F32))
    return y.astype(u.dtype)


def causal_depthwise_conv(x, w):
    ch = x.shape[-1]
    return lax.conv_general_dilated(
        x, w[:, None, :].astype(x.dtype), window_strides=(1,),
        padding=[(CONV_K - 1, 0)], dimension_numbers=('NWC', 'WIO', 'NWC'),
        feature_group_count=ch)


def rotate_every_two(x, pos):
    d = x.shape[-1]
    inv = 1.0 / (10000.0 ** jnp.linspace(0.0, 1.0, d // 2, dtype=F32))
    ang = pos.astype(F32)[:, None] * inv[None, :]
    cos = jnp.cos(ang)[None, :, None, :]
    sin = jnp.sin(ang)[None, :, None, :]
    x1, x2 = x[..., 0::2], x[..., 1::2]
    return jnp.stack([x1 * cos - x2 * sin, x1 * sin + x2 * cos], axis=-1).reshape(x.shape)


def retention_chunkwise(q, k, v):
    b, l, h, d = q.shape
    n = l // RET_CHUNK
    log_gamma = jnp.log(1.0 - 2.0 ** (-5.0 - jnp.arange(h, dtype=F32)))
    idx = jnp.arange(RET_CHUNK, dtype=F32)
    rel = idx[:, None] - idx[None, :]
    inner_decay = jnp.where(rel >= 0,
                            jnp.exp(log_gamma[:, None, None] * jnp.maximum(rel, 0.0)), 0.0)
    query_decay = jnp.exp(log_gamma[:, None] * (idx + 1.0))[None, :, :, None]
    key_decay = jnp.exp(log_gamma[:, None] * (RET_CHUNK - 1.0 - idx))[None, :, :, None]
    chunk_decay = jnp.exp(log_gamma * RET_CHUNK)[None, :, None, None]

    def to_chunks(t):
        return t.reshape(b, n, RET_CHUNK, h, t.shape[-1]).transpose(1, 0, 3, 2, 4)

    qc, kc, vc = to_chunks(q), to_chunks(k * d ** -0.5), to_chunks(v)

    def step(state, inp):
        q_i, k_i, v_i = inp
        s = jnp.einsum('bhqd,bhkd->bhqk', q_i, k_i) * inner_decay
        inner = jnp.einsum('bhqk,bhkv->bhqv', s, v_i)
        cross = jnp.einsum('bhqd,bhdv->bhqv', q_i, state) * query_decay
        new_state = state * chunk_decay + jnp.einsum('bhkd,bhkv->bhdv', k_i * key_decay, v_i)
        return new_state, inner + cross

    state0 = jnp.zeros((b, h, d, v.shape[-1]), F32)
    _, out = lax.scan(step, state0, (qc, kc, vc))
    return out.transpose(1, 0, 3, 2, 4).reshape(b, l, h, v.shape[-1])


def even_mixer(h, w_in, b_forget, log_dt, lam_re, lam_im, b_re, b_im, c_re, c_im,
               d_skip, w_glu, b_glu, w_out):
    b, l, _ = h.shape
    proj = h @ w_in
    q, k, v, f_logit, u = jnp.split(
        proj, [FOX_WIDTH, 2 * FOX_WIDTH, 3 * FOX_WIDTH, 3 * FOX_WIDTH + FOX_HEADS], axis=-1)
    heads = lambda t: t.reshape(b, l, FOX_HEADS, FOX_HEAD_DIM)
    log_f = jax.nn.log_sigmoid(f_logit.astype(F32) + b_forget.astype(F32))
    fox = forgetting_attention(heads(q), heads(k), heads(v), log_f)
    s5 = s5_layer(u, log_dt, lam_re, lam_im, b_re, b_im, c_re, c_im, d_skip, w_glu, b_glu)
    return jnp.concatenate([fox.astype(h.dtype), s5], axis=-1) @ w_out


def odd_mixer(h, w_in, conv_w, w_out):
    b, l, _ = h.shape
    proj = h @ w_in
    cw, rw = CONV_WIDTH, RET_WIDTH
    hc, gate_b, gate_c, q, k, v, g = jnp.split(
        proj, [cw, 2 * cw, 3 * cw, 3 * cw + rw, 3 * cw + 2 * rw, 3 * cw + 3 * rw], axis=-1)
    conv_out = gate_b * causal_depthwise_conv(gate_c * hc, conv_w)
    pos = jnp.arange(l)
    heads = lambda t: t.astype(F32).reshape(b, l, RET_HEADS, RET_HEAD_DIM)
    ret = retention_chunkwise(rotate_every_two(heads(q), pos), rotate_every_two(heads(k), pos), heads(v))
    mu = jnp.mean(ret, axis=-1, keepdims=True)
    var = jnp.mean(jnp.square(ret - mu), axis=-1, keepdims=True)
    ret = ((ret - mu) * lax.rsqrt(var + EPS)).reshape(b, l, rw)
    ret_out = (jax.nn.silu(g.astype(F32)) * ret).astype(h.dtype)
    return jnp.concatenate([conv_out, ret_out], axis=-1) @ w_out


def squared_relu_mlp(h, w_up, w_down):
    return jnp.square(jax.nn.relu(h @ w_up)) @ w_down


def setup_inputs(seed: int = 0) -> dict:
    key = jax.random.key(seed)
    ks = jax.random.split(key, 32)
    nrm = lambda k, shape, s: jax.random.normal(k, shape, F32) * s
    gain = lambda k, shape: 1.0 + 0.02 * jax.random.normal(k, shape, F32)
    x = jax.random.normal(ks[0], (BATCH, SEQ, D_MODEL), F32)
    lam_im_base = math.pi * jnp.arange(S5_STATE, dtype=F32)
    return {
        "x": x,
        "even_norm_mix": gain(ks[1], (N_EVEN, D_MODEL)),
        "even_w_in": nrm(ks[2], (N_EVEN, D_MODEL, EVEN_IN), D_MODEL ** -0.5),
        "even_b_forget": 3.0 + nrm(ks[3], (N_EVEN, FOX_HEADS), 0.1),
        "even_s5_log_dt": jax.random.uniform(ks[4], (N_EVEN, S5_GROUPS), F32,
                                              math.log(1e-3), math.log(1e-1)),
        "even_s5_lambda_re": -0.5 * jnp.exp(nrm(ks[5], (N_EVEN, S5_GROUPS, S5_STATE), 0.02)),
        "even_s5_lambda_im": lam_im_base + nrm(ks[6], (N_EVEN, S5_GROUPS, S5_STATE), 0.01),
        "even_s5_b_re": nrm(ks[7], (N_EVEN, S5_GROUPS, S5_STATE, S5_GROUP), (2 * S5_GROUP) ** -0.5),
        "even_s5_b_im": nrm(ks[8], (N_EVEN, S5_GROUPS, S5_STATE, S5_GROUP), (2 * S5_GROUP) ** -0.5),
        "even_s5_c_re": nrm(ks[9], (N_EVEN, S5_GROUPS, S5_GROUP, S5_STATE), (2 * S5_STATE) ** -0.5 * 4.0),
        "even_s5_c_im": nrm(ks[10], (N_EVEN, S5_GROUPS, S5_GROUP, S5_STATE), (2 * S5_STATE) ** -0.5 * 4.0),
        "even_s5_d": nrm(ks[11], (N_EVEN, S5_WIDTH), 1.0),
        "even_s5_w_glu": nrm(ks[12], (N_EVEN, S5_WIDTH, S5_WIDTH), S5_WIDTH ** -0.5),
        "even_s5_b_glu": nrm(ks[13], (N_EVEN, S5_WIDTH), 0.01),
        "even_w_out": nrm(ks[14], (N_EVEN, D_MODEL, D_MODEL), D_MODEL ** -0.5),
        "odd_norm_mix": gain(ks[15], (N_ODD, D_MODEL)),
        "odd_w_in": nrm(ks[16], (N_ODD, D_MODEL, ODD_IN), D_MODEL ** -0.5),
        "odd_conv_w": nrm(ks[17], (N_ODD, CONV_K, CONV_WIDTH), CONV_K ** -0.5),
        "odd_w_out": nrm(ks[18], (N_ODD, D_MODEL, D_MODEL), D_MODEL ** -0.5),
        "mlp_norm": gain(ks[19], (DEPTH, D_MODEL)),
        "mlp_w_up": nrm(ks[20], (DEPTH, D_MODEL, D_FF), D_MODEL ** -0.5),
        "mlp_w_down": nrm(ks[21], (DEPTH, D_FF, D_MODEL), D_FF ** -0.5),
        "final_norm": gain(ks[22], (D_MODEL,)),
    }


def reference(x, even_norm_mix, even_w_in, even_b_forget, even_s5_log_dt, even_s5_lambda_re,
              even_s5_lambda_im, even_s5_b_re, even_s5_b_im, even_s5_c_re, even_s5_c_im,
              even_s5_d, even_s5_w_glu, even_s5_b_glu, even_w_out, odd_norm_mix, odd_w_in,
              odd_conv_w, odd_w_out, mlp_norm, mlp_w_up, mlp_w_down, final_norm):
    for layer in range(DEPTH):
        j = layer // 2
        if layer % 2 == 0:
            x = x + even_mixer(rms_norm(x, even_norm_mix[j]), even_w_in[j], even_b_forget[j],
                               even_s5_log_dt[j], even_s5_lambda_re[j], even_s5_lambda_im[j],
                               even_s5_b_re[j], even_s5_b_im[j], even_s5_c_re[j], even_s5_c_im[j],
                               even_s5_d[j], even_s5_w_glu[j], even_s5_b_glu[j], even_w_out[j])
        else:
            x = x + odd_mixer(rms_norm(x, odd_norm_mix[j]), odd_w_in[j], odd_conv_w[j], odd_w_out[j])
        x = x + squared_relu_mlp(rms_norm(x, mlp_norm[layer]), mlp_w_up[layer], mlp_w_down[layer])
    return rms_norm(x, final_norm)
```

```python
import contextlib
import numpy as np
import concourse.bass as bass
import concourse.mybir as mybir

F32 = mybir.dt.float32
BF16 = mybir.dt.bfloat16
ALU = mybir.AluOpType
AF = mybir.ActivationFunctionType
AX = mybir.AxisListType

SAME_ENGINE_SYNC = True


class Prog:
    ENG = ['pe', 'act', 'dve', 'pool', 'sp']
    MAIN = ['pe', 'act', 'dve', 'sp']

    def __init__(self, nc, stack, nds_main=24, nds_pool=6):
        self.nc = nc
        self.eng = {'pe': nc.tensor, 'act': nc.scalar, 'dve': nc.vector,
                    'pool': nc.gpsimd, 'sp': nc.sync}
        self.ops = {e: [] for e in self.ENG}
        self.sem = {e: stack.enter_context(nc.semaphore('s_' + e)) for e in self.ENG}
        nds = nds_main + nds_pool
        self.dsem = [stack.enter_context(nc.semaphore('d%d' % i)) for i in range(nds)]
        self.dcnt = [0] * nds
        self.grp = {'main': list(range(nds_main)), 'pool': list(range(nds_main, nds))}
        self.gnext = {'main': 0, 'pool': 0}
        self.res = {}

    def _deps(self, reads, writes):
        deps = set()
        for k in reads:
            r = self.res.get(k)
            if r:
                deps.update(r[0])
        for k in writes:
            r = self.res.get(k)
            if r:
                deps.update(r[0])
                for e, p in r[1].items():
                    deps.add(('e', e, p))
                deps.update(r[2])
        return deps

    def _mark(self, tok, reads, writes, acc=False):
        for k in reads:
            r = self.res.setdefault(k, [[], {}, []])
            if tok[0] == 'e':
                r[1][tok[1]] = tok[2]
            else:
                r[2].append(tok)
        for k in writes:
            if acc and k in self.res:
                self.res[k][0] = [t for t in self.res[k][0] if not (t[0] == 'd' and tok[0] == 'd' and t[1] == tok[1])] + [tok]
            else:
                self.res[k] = [[tok], {}, []]

    def op(self, e, fn, kw=None, r=(), w=(), sig=True):
        if isinstance(fn, str):
            meth, kws = fn, dict(kw)
            fn = lambda eng: getattr(eng, meth)(**kws)
        deps = self._deps(r, w)
        pos = len(self.ops[e])
        tok = ('e', e, pos)
        self.ops[e].append(dict(fn=fn, deps=deps, sig=sig, dma=None))
        self._mark(tok, r, w)
        return tok

    def dma(self, q, out, in_, r=(), w=(), grp=None, acc=False, **kw):
        grp = grp or ('pool' if q == 'pool' else 'main')
        lst = self.grp[grp]
        i = lst[self.gnext[grp] % len(lst)]
        self.gnext[grp] += 1
        deps = self._deps(r, w)
        if self.dcnt[i] > 0:
            deps.add(('d', i, self.dcnt[i]))
        self.dcnt[i] += 16
        tok = ('d', i, self.dcnt[i])
        self.ops[q].append(dict(fn=lambda eng: eng.dma_start(out=out, in_=in_, **kw),
                                deps=deps, sig=False, dma=i))
        self._mark(tok, r, w, acc)
        return tok

    def barrier(self, engs=None):
        engs = engs or self.MAIN
        toks = set()
        for e in engs:
            for p in range(len(self.ops[e]) - 1, -1, -1):
                o = self.ops[e][p]
                if o['sig'] and o['dma'] is None and o['fn'] is not None:
                    toks.add(('e', e, p))
                    break
        for i in self.grp['main']:
            if self.dcnt[i] > 0:
                toks.add(('d', i, self.dcnt[i]))
        for e in engs:
            self.ops[e].append(dict(fn=None, deps=set(toks), sig=False, dma=None))

    def emit(self):
        nc = self.nc
        cum = {}
        for e in self.ENG:
            c = 0
            arr = []
            for o in self.ops[e]:
                if o['sig'] and o['dma'] is None:
                    c += 1
                arr.append(c)
            need = [0] * len(arr)
            nxt = None
            for p in range(len(arr) - 1, -1, -1):
                o = self.ops[e][p]
                if o['sig'] and o['dma'] is None:
                    nxt = arr[p]
                need[p] = nxt
            cum[e] = need
        self.total = {e: (cum[e] and max([x for x in cum[e] if x is not None] or [0])) for e in self.ENG}

        def resolve(tok):
            if tok[0] == 'd':
                return self.dsem[tok[1]], ('d', tok[1]), tok[2]
            _, e, p = tok
            v = cum[e][p]
            assert v is not None, ("no signalling op after", tok)
            return self.sem[e], ('e', e), v

        with nc.Block() as block:
            for e in self.ENG:
                ops = self.ops[e]
                if not ops:
                    continue

                def body(engine, e=e, ops=ops):
                    seen = {}
                    for pos, o in enumerate(ops):
                        for tok in sorted(o['deps']):
                            if tok[0] == 'e' and tok[1] == e:
                                if e in ('pe', 'sp') or (not SAME_ENGINE_SYNC and o['dma'] is None):
                                    continue
                                if tok[2] >= pos:
                                    continue
                            sem, key, val = resolve(tok)
                            if seen.get(key, 0) >= val:
                                continue
                            engine.wait_ge(sem, val)
                            seen[key] = val
                        if o['fn'] is None:
                            continue
                        ins = o['fn'](engine)
                        if o['dma'] is not None:
                            ins.then_inc(self.dsem[o['dma']], 16)
                        elif o['sig']:
                            ins.then_inc(self.sem[e], 1)

                getattr(block, {'pe': 'tensor', 'act': 'scalar', 'dve': 'vector',
                                'pool': 'gpsimd', 'sp': 'sync'}[e])(body)

from concourse.bass_utils import run_bass_kernel_spmd
import ml_dtypes
import math

NT = 4096
EPS = 1e-6
PI = math.pi


def _ap(t, off, pat):
    return bass.AP(t.tensor, t.offset + off, [list(p) for p in pat])


def build(phases=None, dbg=(), feed=()):
    phases = set(range(1, 11)) if phases is None else set(phases)
    nc = bass.Bass("TRN2", target_bir_lowering=False)
    D = {}

    def din(n, shape, dt=F32):
        D[n] = nc.dram_tensor(n, list(shape), dt, kind="ExternalInput").ap()

    def dsc(n, shape, dt=BF16):
        kind = "ExternalInput" if n in feed else ("ExternalOutput" if n in dbg else "Internal")
        D[n] = nc.dram_tensor(n, list(shape), dt, kind=kind).ap()

    din("x", [NT, 1024]); din("g_even", [1024]); din("w_in0", [1024, 2056]); din("b_forget", [8])
    din("log_dt", [32]); din("lam_re", [32, 64]); din("lam_im", [32, 64])
    din("b_re", [32, 64, 16]); din("b_im", [32, 64, 16]); din("c_re", [32, 16, 64]); din("c_im", [32, 16, 64])
    din("d_skip", [512]); din("w_glu", [512, 512]); din("b_glu", [512]); din("w_out0", [1024, 1024])
    din("g_odd", [1024]); din("w_in1", [1024, 3584]); din("conv_w", [3, 512]); din("w_out1", [1024, 1024])
    din("g_mlp", [2, 1024]); din("w_up", [2, 1024, 4096]); din("w_down", [2, 4096, 1024]); din("g_final", [1024])
    din("c_identb", [128, 128], BF16); din("c_identf", [128, 128]); din("c_trib", [128, 128], BF16)
    din("c_rcos", [NT, 64]); din("c_rsin", [NT, 64]); din("c_qdec", [128, 4]); din("c_kdec", [128, 4])
    din("c_rmask", [128, 512]); din("c_iota", [128, 512]); din("c_cmask", [128, 128])
    D["out"] = nc.dram_tensor("out", [NT, 1024], F32, kind="ExternalOutput").ap()
    for n, sh in [("wb_in0", [1024, 2056]), ("wb_glu", [512, 512]), ("wb_out0", [1024, 1024]), ("wb_up0", [1024, 4096]),
                  ("wb_dn0", [4096, 1024]), ("wb_in1", [1024, 3584]), ("wb_out1", [1024, 1024]), ("wb_up1", [1024, 4096]),
                  ("wb_dn1", [4096, 1024]), ("hT0", [1024, NT]), ("hT1", [1024, NT]), ("hT2", [1024, NT]), ("hT3", [1024, NT]),
                  ("qT", [512, NT]), ("kT", [512, NT]), ("uT", [512, NT]), ("vaug", [NT, 1024]),
                  ("oT", [512, NT]), ("s5T", [512, NT]), ("convT", [512, NT]), ("retT", [512, NT]),
                  ("qd", [NT, 512]), ("kd", [NT, 512]), ("vt", [NT, 512]), ("sg", [NT, 512])]:
        dsc(n, sh)
    dsc("cT", [8, NT], F32); dsc("xr1", [NT, 1024], F32); dsc("xr2", [NT, 1024], F32); dsc("xr3", [NT, 1024], F32)

    st = contextlib.ExitStack()
    ncd = st.enter_context(nc.allow_non_contiguous_dma(reason="small param loads"))
    P = Prog(nc, st)
    _uid = [0]

    def sbt(stk, name, shape, dt=F32):
        _uid[0] += 1
        return stk.enter_context(nc.sbuf_tensor("%s_u%d" % (name, _uid[0]), list(shape), dt))

    def MM(out, lhsT, rhs, start, stop, r, w, sig=True):
        P.op('pe', 'matmul', dict(out=out, lhsT=lhsT, rhs=rhs, start=start, stop=stop), r, w, sig)

    def TR(out, in_, ident, r, w, sig=True):
        P.op('pe', 'transpose', dict(out=out, in_=in_, identity=ident), r, w, sig)

    def ACT(out, in_, func, r, w, **kw):
        P.op('act', 'activation', dict(out=out, in_=in_, func=func, **kw), r, w)

    def TT(eng, out, in0, in1, op, r, w):
        P.op(eng, 'tensor_tensor', dict(out=out, in0=in0, in1=in1, op=op), r, w)

    def TS(eng, out, in0, s1, s2, op0, op1, r, w):
        kw = dict(out=out, in0=in0, scalar1=s1, scalar2=s2, op0=op0)
        if op1 is not None:
            kw['op1'] = op1
        P.op(eng, 'tensor_scalar', kw, r, w)

    def STT(eng, out, in0, scalar, in1, op0, op1, r, w):
        P.op(eng, 'scalar_tensor_tensor', dict(out=out, in0=in0, scalar=scalar, in1=in1, op0=op0, op1=op1), r, w)

    def CP(eng, out, in_, r, w):
        P.op(eng, 'tensor_copy', dict(out=out, in_=in_), r, w)

    def MS(eng, ap, val, w):
        P.op(eng, 'memset', dict(ap=ap, constant=val), (), w)

    identb = sbt(st, "identb", [128, 128], BF16); identf = sbt(st, "identf", [128, 128]); trib = sbt(st, "trib", [128, 128], BF16)
    gains = sbt(st, "gains", [128, 4, 8])
    stg = [sbt(st, "stg%d" % i, [128, 2048]) for i in range(2)]
    wo = [sbt(st, "wo%d" % i, [128, 2048], BF16) for i in range(2)]
    psf = [st.enter_context(nc.psum_tensor("psf%d" % i, [128, 512], F32)) for i in range(6)]
    psb = [st.enter_context(nc.psum_tensor("psb%d" % i, [128, 1024], BF16)) for i in range(2)]
    cnt = {"psf": 0, "psb": 0, "nrm": 0, "ev": 0}

    def psn(lo=0, hi=6):
        i = lo + cnt["psf"] % (hi - lo); cnt["psf"] += 1
        return psf[i], "psf%d" % i

    def psbn():
        i = cnt["psb"] % 2; cnt["psb"] += 1
        return psb[i], "psb%d" % i

    P.dma('sp', identb[:], D["c_identb"], w=["identb"]); P.dma('sp', identf[:], D["c_identf"], w=["identf"])
    P.dma('sp', trib[:], D["c_trib"], w=["trib"])
    for wi, src in enumerate([D["g_even"], D["g_mlp"][0], D["g_odd"], D["g_mlp"][1]]):
        P.dma('pool', gains[:, wi, :], src.rearrange("(k p) -> p k", p=128), w=["gains"], acc=True)

    wpc = [0]

    def wprep(src, dstn, K, N, gi=None):
        dst = D[dstn]
        for k in range(K // 128):
            for c0 in range(0, N, 2048):
                n = min(N, c0 + 2048) - c0
                b = wpc[0] % 2; wpc[0] += 1
                P.dma('pool', stg[b][:, :n], src[k * 128:(k + 1) * 128, c0:c0 + n], w=["stg%d" % b])
                if gi is None:
                    CP('pool', wo[b][:, :n], stg[b][:, :n], ["stg%d" % b], ["wo%d" % b])
                else:
                    TS('pool', wo[b][:, :n], stg[b][:, :n], gains[:, gi, k:k + 1], None, ALU.mult, None, ["stg%d" % b, "gains"], ["wo%d" % b])
                P.dma('pool', dst[k * 128:(k + 1) * 128, c0:c0 + n], wo[b][:, :n], r=["wo%d" % b], w=[dstn], acc=True)

    if 2 in phases: wprep(D["w_in0"], "wb_in0", 1024, 2056)
    if 4 in phases: wprep(D["w_glu"], "wb_glu", 512, 512)
    if 5 in phases: wprep(D["w_out0"], "wb_out0", 1024, 1024)
    if 6 in phases:
        wprep(D["w_up"][0], "wb_up0", 1024, 4096)
        wprep(D["w_down"][0], "wb_dn0", 4096, 1024)
    if 7 in phases: wprep(D["w_in1"], "wb_in1", 1024, 3584)
    if 9 in phases: wprep(D["w_out1"], "wb_out1", 1024, 1024)
    if 10 in phases:
        wprep(D["w_up"][1], "wb_up1", 1024, 4096)
        wprep(D["w_down"][1], "wb_dn1", 4096, 1024)

    def load_w(tile, name, K, nsplit=2):
        src = D[name].rearrange("(k p) n -> p k n", p=128)
        kk = K // 128
        step = max(1, kk // nsplit)
        for k0 in range(0, kk, step):
            P.dma('sp', tile[:, k0:k0 + step, :], src[:, k0:k0 + step, :], r=[name], w=["W_" + name], acc=True)
        return "W_" + name

    def mk_norm(stk, gname=None, goff=0):
        gv = None
        if gname is not None:
            gv = sbt(stk, "ngv", [128, 1024])
            P.dma('sp', gv[:], _ap(D[gname], goff, [[0, 128], [1, 1024]]), w=["ngv"])
        return dict(gv=gv, junk=sbt(stk, "njunk", [128, 1024], BF16), ss=sbt(stk, "nss", [128, 64]), sd=sbt(stk, "nsd", [128, 64]),
                    rs=sbt(stk, "nrs", [128, 64]), hb=[sbt(stk, "nhb%d" % i, [128, 1024], BF16) for i in range(2)])

    def norm_stats(N_, xt, xkey):
        j = cnt["nrm"] % 64; cnt["nrm"] += 1
        ACT(N_["junk"][:], xt, AF.Square, [xkey], ["njunk", "nss%d" % j], accum_out=N_["ss"][:, j:j + 1])
        ACT(N_["sd"][:, j:j + 1], N_["ss"][:, j:j + 1], AF.Sqrt, ["nss%d" % j], ["nsd%d" % j], bias=EPS, scale=1.0 / 1024)
        P.op('dve', 'reciprocal', dict(out=N_["rs"][:, j:j + 1], in_=N_["sd"][:, j:j + 1]), ["nsd%d" % j], ["nrs%d" % j])
        return j

    def norm_A(N_, xt, xkey):
        j = norm_stats(N_, xt, xkey)
        b = j % 2
        STT('dve', N_["hb"][b][:], xt, N_["rs"][:, j:j + 1], N_["gv"][:], ALU.mult, ALU.mult, [xkey, "nrs%d" % j, "ngv"], ["nhb%d" % b])
        return b

    def norm_B(N_, b, hTs, hkey, col):
        pt, pk = psbn()
        for k in range(8):
            TR(pt[:, k * 128:(k + 1) * 128], N_["hb"][b][:, k * 128:(k + 1) * 128], identb[:], ["nhb%d" % b, "identb"], [pk], sig=(k == 7))
        P.op('act', 'copy', dict(out=hTs[:, :, col * 128:(col + 1) * 128], in_=pt[:].rearrange("p (k t) -> p k t", k=8)), [pk], [hkey])

    def norm_T(N_, xt, xkey, hTs, hkey, col):
        b = norm_A(N_, xt, xkey)
        norm_B(N_, b, hTs, hkey, col)

    def evac(out, in_, r, w):
        cnt["ev"] += 1
        if cnt["ev"] % 2 == 0:
            P.op('act', 'copy', dict(out=out, in_=in_), r, w)
        else:
            CP('dve', out, in_, r, w)

    hTv = lambda n: D[n].rearrange("(k p) t -> p k t", p=128)

    if 1 in phases:
        with contextlib.ExitStack() as ph:
            N_ = mk_norm(ph, "g_even")
            xt = [sbt(ph, "xt%d" % i, [128, 1024]) for i in range(2)]
            hTs = [sbt(ph, "hTs%d" % i, [128, 8, 512], BF16) for i in range(2)]
            P.dma('sp', xt[0][:], D["x"][0:128, :], w=["xt0"])
            for tb in range(32):
                b = tb % 2; hb_ = (tb // 4) % 2
                if tb + 1 < 32:
                    P.dma('sp', xt[1 - b][:], D["x"][(tb + 1) * 128:(tb + 2) * 128, :], w=["xt%d" % (1 - b)])
                norm_T(N_, xt[b][:], "xt%d" % b, hTs[hb_], "hTs%d" % hb_, tb % 4)
                if tb % 4 == 3:
                    blk = tb // 4
                    P.dma('sp', hTv("hT0")[:, :, blk * 512:(blk + 1) * 512], hTs[hb_][:], r=["hTs%d" % hb_], w=["hT0"], acc=True)
            P.barrier()

    if 2 in phases:
        with contextlib.ExitStack() as ph:
            W = sbt(ph, "W2", [128, 8, 2056], BF16); wk = load_w(W, "wb_in0", 1024)
            hb = [sbt(ph, "hblk%d" % i, [128, 8, 512], BF16) for i in range(2)]
            ost = [sbt(ph, "ost%d" % i, [128, 12, 512], BF16) for i in range(2)]
            vst = [sbt(ph, "vst%d" % i, [128, 4, 8, 128], BF16) for i in range(2)]
            negb = sbt(ph, "negb", [8, 1]); fsp = sbt(ph, "fsp", [8, NT]); ftmp = sbt(ph, "ftmp", [8, 512]); one8 = sbt(ph, "one8", [8, 1])
            cTs = sbt(ph, "cTs", [8, NT])
            P.dma('sp', negb[:], D["b_forget"].rearrange("(h o) -> h o", o=1), w=["negb"])
            TS('dve', negb[:], negb[:], -1.0, None, ALU.mult, None, ["negb"], ["negb"])
            MS('dve', one8[:], 1.0, ["one8"])
            for i in range(2):
                MS('dve', vst[i][:], 1.0, ["vst%d" % i])
            fcols = [0, 128, 256, 384, 512, 640, 768, 896, 1544, 1672, 1800, 1928]
            P.dma('sp', hb[0][:], hTv("hT0")[:, :, 0:512], r=["hT0"], w=["hblk0"])
            for blk in range(8):
                b = blk % 2; hk = "hblk%d" % b; cs = slice(blk * 512, (blk + 1) * 512)
                if blk + 1 < 8:
                    P.dma('sp', hb[1 - b][:], hTv("hT0")[:, :, (blk + 1) * 512:(blk + 2) * 512], r=["hT0"], w=["hblk%d" % (1 - b)])
                for ft in range(12):
                    ps, pk = psn()
                    for k in range(8):
                        MM(ps[:, :], W[:, k, fcols[ft]:fcols[ft] + 128], hb[b][:, k, :], k == 0, k == 7, [wk, hk], [pk], sig=(k == 7))
                    evac(ost[b][:, ft, :], ps[:, :], [pk], ["ost%d_%d" % (b, ft // 4)])
                for gi, nm in enumerate(["qT", "kT", "uT"]):
                    P.dma('sp', hTv(nm)[:, :, cs], ost[b][:, gi * 4:(gi + 1) * 4, :], r=["ost%d_%d" % (b, gi)], w=[nm], acc=True)
                ps, pk = psn()
                for k in range(8):
                    MM(ps[:, :], W[:, k, 1536:1664], hb[b][:, k, :], k == 0, k == 7, [wk, hk], [pk], sig=(k == 7))
                ACT(ftmp[:], ps[0:8, :], AF.Exp, [pk, "negb"], ["ftmp"], bias=negb[:, 0:1], scale=-1.0)
                ACT(fsp[:, cs], ftmp[:], AF.Ln, ["ftmp"], ["fsp"], bias=1.0, scale=1.0)
                for tt in range(4):
                    ps, pk = psn()
                    for k in range(8):
                        MM(ps[:, :], hb[b][:, k, tt * 128:(tt + 1) * 128], W[:, k, 1024:1536], k == 0, k == 7, [wk, hk], [pk], sig=(k == 7))
                    evac(vst[b][:, tt, :, 0:64], ps[:, :].rearrange("p (h d) -> p h d", h=8), [pk], ["vst%d" % b])
                P.dma('sp', D["vaug"][cs, :].rearrange("(t p) c -> p t c", p=128), vst[b][:].rearrange("p t h d -> p t (h d)"), r=["vst%d" % b], w=["vaug"], acc=True)
            P.op('dve', 'tensor_tensor_scan', dict(out=cTs[:], data0=one8[:, 0:1].to_broadcast([8, NT]), data1=fsp[:], initial=0.0, op0=ALU.mult, op1=ALU.subtract),
                 ["one8", "fsp"], ["cTs"])
            P.dma('sp', D["cT"], cTs[:], r=["cTs"], w=["cT"])
            P.barrier()

    if 3 in phases:
        with contextlib.ExitStack() as ph:
            qTs = sbt(ph, "qTs", [128, 4, NT], BF16); kTs = sbt(ph, "kTs", [128, 4, NT], BF16)
            vs = sbt(ph, "vs", [128, 32, 1024], BF16)
            ccol = sbt(ph, "ccol", [128, 32, 8]); R = sbt(ph, "R", [128, 8, 8]); biasT = sbt(ph, "biasT", [128, 32, 8, 8])
            Pt = [sbt(ph, "Pt%d" % i, [128, 512], BF16) for i in range(4)]
            qm = [sbt(ph, "qm%d" % i, [128, 512], BF16) for i in range(2)]
            rd = sbt(ph, "rd", [128, 512]); fost = [sbt(ph, "fost%d" % i, [128, NT], BF16) for i in range(2)]
            for k in range(4):
                P.dma('sp', qTs[:, k, :], D["qT"][k * 128:(k + 1) * 128, :], r=["qT"], w=["qTs"], acc=True)
                P.dma('sp', kTs[:, k, :], D["kT"][k * 128:(k + 1) * 128, :], r=["kT"], w=["kTs"], acc=True)
            vv = D["vaug"].rearrange("(t p) c -> p t c", p=128)
            for t0 in range(0, 32, 8):
                P.dma('sp', vs[:, t0:t0 + 8, :], vv[:, t0:t0 + 8, :], r=["vaug"], w=["vs"], acc=True)
            for hh in range(8):
                P.dma('sp', ccol[:, :, hh], D["cT"][hh].rearrange("(j p) -> p j", p=128), r=["cT"], w=["ccol"], acc=True)
                P.dma('sp', R[:, :, hh], _ap(D["cT"], hh * NT, [[0, 128], [512, 8]]), r=["cT"], w=["R"], acc=True)
            for i in range(32):
                TT('dve', biasT[:, i, :, :], R[:], _ap(ccol[:], i * 8, [list(ccol[:].ap[0]), [0, 8], [1, 8]]), ALU.subtract, ["R", "ccol"], ["biasT"])
            pcount = 0
            for h in range(8):
                kt = h // 2; pb = (h % 2) * 64
                for J in range(8):
                    O, ok = psn(0, 2)
                    ni = 4 * J + 4
                    Ss = {}
                    qb = (h * 8 + J) % 2; qk = "qm%d" % qb
                    MS('dve', qm[qb][:], 0.0, [qk])
                    CP('dve', qm[qb][pb:pb + 64, :], qTs[pb:pb + 64, kt, J * 512:(J + 1) * 512], ["qTs", qk], [qk])

                    def issue_S(i):
                        c0 = max(i - 4 * J, 0) * 128
                        S, sk = psn(2, 6)
                        MM(S[:, c0:512], kTs[:, kt, i * 128:(i + 1) * 128], qm[qb][:, c0:512], True, True, ["kTs", qk], [sk])
                        Ss[i] = (S, sk, c0)
                    issue_S(0)
                    issue_S(1)
                    for i in range(ni):
                        if i + 2 < ni:
                            issue_S(i + 2)
                        S, sk, c0 = Ss.pop(i)
                        pb_ = pcount % 4; pcount += 1; pkey = "Pt%d" % pb_
                        ACT(Pt[pb_][:, c0:512], S[:, c0:512], AF.Exp, [sk, "biasT"], [pkey], bias=biasT[:, i, J, h:h + 1], scale=0.125)
                        if i >= 4 * J:
                            TT('dve', Pt[pb_][:, c0:c0 + 128], Pt[pb_][:, c0:c0 + 128], trib[:], ALU.mult, [pkey, "trib"], [pkey])
                        MM(O[:, c0:512], vs[:, i, h * 128:(h + 1) * 128], Pt[pb_][:, c0:512], i == 0, i == ni - 1, ["vs", pkey], [ok], sig=(i == ni - 1))
                    P.op('dve', 'reciprocal', dict(out=rd[64:128, :], in_=O[64:128, :]), [ok], ["rd"])
                    TT('dve', fost[kt % 2][pb:pb + 64, J * 512:(J + 1) * 512], O[0:64, :], rd[64:128, :], ALU.mult, [ok, "rd"], ["fost%d" % (kt % 2)])
                if h % 2 == 1:
                    P.dma('sp', D["oT"][kt * 128:(kt + 1) * 128, :], fost[kt % 2][:], r=["fost%d" % (kt % 2)], w=["oT"], acc=True)
            P.barrier()

    if 4 in phases:
        with contextlib.ExitStack() as ph:
            f2 = lambda n, sh, dt=F32: sbt(ph, n, sh, dt)
            lre = f2("lre", [128, 16]); lim = f2("lim", [128, 16]); ldt = f2("ldt", [128, 16]); dtt = f2("dtt", [128, 16])
            are = f2("are", [128, 16]); th = f2("th", [128, 16]); mag = f2("mag", [128, 16])
            cs1 = f2("cs1", [128, 16]); sn1 = f2("sn1", [128, 16]); c512 = f2("c512", [128, 16]); s512 = f2("s512", [128, 16]); ns512 = f2("ns512", [128, 16])
            tmpa = f2("tmpa", [128, 16]); tmpb = f2("tmpb", [128, 16]); nr = f2("nr", [128, 16]); ni_ = f2("ni_", [128, 16])
            den = f2("den", [128, 16]); cr = f2("cr", [128, 16]); ci = f2("ci", [128, 16])
            braw = f2("braw", [128, 16, 16]); biraw = f2("biraw", [128, 16, 16]); Bre = f2("Bre", [128, 16, 16]); Bim = f2("Bim", [128, 16, 16]); btmp = f2("btmp", [128, 16, 16])
            BreT = f2("BreT", [128, 16, 128], BF16); BimT = f2("BimT", [128, 16, 128], BF16)
            CWr = f2("CWr", [128, 16, 128], BF16); NCr = f2("NCr", [128, 16, 128], BF16); NCi = f2("NCi", [128, 16, 128], BF16)
            cosA = f2("cosA", [128, 16, 512]); sinA = f2("sinA", [128, 16, 512])
            dsk = f2("dsk", [128, 4]); bgl = f2("bgl", [128, 4]); cmask = f2("cmask", [128, 128])
            Wg = f2("Wg", [128, 4, 512], BF16); wgk = load_w(Wg, "wb_glu", 512, 1)
            P.dma('sp', lre[:], _ap(D["lam_re"], 0, [[1, 128], [128, 16]]), w=["lre"])
            P.dma('sp', lim[:], _ap(D["lam_im"], 0, [[1, 128], [128, 16]]), w=["lim"])
            for gl in range(2):
                P.dma('sp', ldt[gl * 64:(gl + 1) * 64, :], _ap(D["log_dt"], gl, [[0, 64], [2, 16]]), w=["ldt"], acc=True)
            for half in range(2):
                P.dma('sp', braw[:, half * 8:(half + 1) * 8, :], _ap(D["b_re"], half * 8 * 2048, [[16, 128], [2048, 8], [1, 16]]), w=["braw"], acc=True)
                P.dma('sp', biraw[:, half * 8:(half + 1) * 8, :], _ap(D["b_im"], half * 8 * 2048, [[16, 128], [2048, 8], [1, 16]]), w=["biraw"], acc=True)
            P.dma('sp', dsk[:], D["d_skip"].rearrange("(k p) -> p k", p=128), w=["dsk"])
            P.dma('sp', bgl[:], D["b_glu"].rearrange("(k p) -> p k", p=128), w=["bgl"])
            P.dma('sp', cmask[:], D["c_cmask"], w=["cmask"])
            ACT(dtt[:], ldt[:], AF.Exp, ["ldt"], ["dtt"])
            TT('dve', are[:], lre[:], dtt[:], ALU.mult, ["lre", "dtt"], ["are"])
            TT('dve', th[:], lim[:], dtt[:], ALU.mult, ["lim", "dtt"], ["th"])
            ACT(mag[:], are[:], AF.Exp, ["are"], ["mag"])

            kac = f2("kac", [128, 16]); rr = f2("rr", [128, 16]); nrr = f2("nrr", [128, 16])
            TS('dve', kac[:], th[:], PI, None, ALU.is_ge, None, ["th"], ["kac"])
            for m_ in (3.0, 5.0, 7.0):
                TS('dve', tmpa[:], th[:], m_ * PI, None, ALU.is_ge, None, ["th"], ["tmpa"])
                TT('dve', kac[:], kac[:], tmpa[:], ALU.add, ["kac", "tmpa"], ["kac"])
            STT('dve', rr[:], kac[:], -2.0 * PI, th[:], ALU.mult, ALU.add, ["kac", "th"], ["rr"])
            ACT(sn1[:], rr[:], AF.Sin, ["rr"], ["sn1"])
            TS('dve', nrr[:], rr[:], -1.0, None, ALU.mult, None, ["rr"], ["nrr"])
            TT('dve', nrr[:], nrr[:], rr[:], ALU.max, ["nrr", "rr"], ["nrr"])
            TS('dve', nrr[:], nrr[:], -1.0, 0.5 * PI, ALU.mult, ALU.add, ["nrr"], ["nrr"])
            ACT(cs1[:], nrr[:], AF.Sin, ["nrr"], ["cs1"])
            TT('dve', nr[:], mag[:], cs1[:], ALU.mult, ["mag", "cs1"], ["nr"])
            TS('dve', nr[:], nr[:], -1.0, None, ALU.add, None, ["nr"], ["nr"])
            TT('dve', ni_[:], mag[:], sn1[:], ALU.mult, ["mag", "sn1"], ["ni_"])
            TT('dve', den[:], lre[:], lre[:], ALU.mult, ["lre"], ["den"])
            TT('dve', tmpb[:], lim[:], lim[:], ALU.mult, ["lim"], ["tmpb"])
            TT('dve', den[:], den[:], tmpb[:], ALU.add, ["den", "tmpb"], ["den"])
            P.op('dve', 'reciprocal', dict(out=den[:], in_=den[:]), ["den"], ["den"])
            TT('dve', cr[:], nr[:], lre[:], ALU.mult, ["nr", "lre"], ["cr"])
            TT('dve', tmpb[:], ni_[:], lim[:], ALU.mult, ["ni_", "lim"], ["tmpb"])
            TT('dve', cr[:], cr[:], tmpb[:], ALU.add, ["cr", "tmpb"], ["cr"])
            TT('dve', cr[:], cr[:], den[:], ALU.mult, ["cr", "den"], ["cr"])
            TT('dve', ci[:], ni_[:], lre[:], ALU.mult, ["ni_", "lre"], ["ci"])
            TT('dve', tmpb[:], nr[:], lim[:], ALU.mult, ["nr", "lim"], ["tmpb"])
            TT('dve', ci[:], ci[:], tmpb[:], ALU.subtract, ["ci", "tmpb"], ["ci"])
            TT('dve', ci[:], ci[:], den[:], ALU.mult, ["ci", "den"], ["ci"])
            bc = lambda t: _ap(t[:], 0, [list(t[:].ap[0]), [1, 16], [0, 16]])
            TT('dve', Bre[:], braw[:], bc(cr), ALU.mult, ["braw", "cr"], ["Bre"])
            TT('dve', btmp[:], biraw[:], bc(ci), ALU.mult, ["biraw", "ci"], ["btmp"])
            TT('dve', Bre[:], Bre[:], btmp[:], ALU.subtract, ["Bre", "btmp"], ["Bre"])
            TT('dve', Bim[:], biraw[:], bc(cr), ALU.mult, ["biraw", "cr"], ["Bim"])
            TT('dve', btmp[:], braw[:], bc(ci), ALU.mult, ["braw", "ci"], ["btmp"])
            TT('dve', Bim[:], Bim[:], btmp[:], ALU.add, ["Bim", "btmp"], ["Bim"])
            with contextlib.ExitStack() as ph2:
                Bw = [sbt(ph2, "Bw%d" % i, [128, 16, 128]) for i in range(2)]
                cnat = [sbt(ph2, "cnat%d" % i, [128, 4, 64]) for i in range(2)]
                Af = sbt(ph2, "Af", [128, 128])
                for i, (Bsrc, BT, bk) in enumerate(((Bre, BreT, "Bre"), (Bim, BimT, "Bim"))):
                    MS('dve', Bw[i][:], 0.0, ["Bw%d" % i])
                    for pr in range(16):
                        c0 = (pr % 4) * 32
                        CP('dve', Bw[i][0:64, pr, c0:c0 + 16], Bsrc[0:64, pr, :], [bk, "Bw%d" % i], ["Bw%d" % i])
                        CP('dve', Bw[i][64:128, pr, c0 + 16:c0 + 32], Bsrc[64:128, pr, :], [bk, "Bw%d" % i], ["Bw%d" % i])
                    for pr in range(16):
                        ps, pk = psn()
                        TR(ps[:, 0:128], Bw[i][:, pr, :], identf[:], ["Bw%d" % i, "identf"], [pk])
                        evac(BT[:, pr, :], ps[:, 0:128], [pk], ["BT%d" % i])
                for i, cname in enumerate(("c_re", "c_im")):
                    P.dma('sp', cnat[i][:], D[cname].rearrange("g h p -> (g h) p").rearrange("(j r) p -> r j p", r=128), w=["cnat%d" % i])
                for tbl in (CWr, NCr, NCi):
                    MS('dve', tbl[:], 0.0, ["CW"])
                for i in range(2):
                    for j in range(4):
                        CP('dve', Af[:, 0:64], cnat[i][:, j, :], ["cnat%d" % i, "Af"], ["Af"])
                        CP('dve', Af[:, 64:128], cnat[i][:, j, :], ["cnat%d" % i, "Af"], ["Af"])
                        TT('dve', Af[:], Af[:], cmask[:], ALU.mult, ["Af", "cmask"], ["Af"])
                        ps, pk = psn()
                        TR(ps[:, 0:128], Af[:], identf[:], ["Af", "identf"], [pk])
                        for pp in range(4):
                            pr = 4 * j + pp; c0 = pp * 32
                            if i == 0:
                                CP('dve', CWr[:, pr, c0:c0 + 32], ps[:, c0:c0 + 32], [pk, "CW"], ["CW"])
                                TS('dve', NCr[:, pr, c0:c0 + 32], ps[:, c0:c0 + 32], -1.0, None, ALU.mult, None, [pk, "CW"], ["CW"])
                            else:
                                TS('dve', NCi[:, pr, c0:c0 + 32], ps[:, c0:c0 + 32], -1.0, None, ALU.mult, None, [pk, "CW"], ["CW"])
                cE = sbt(ph2, "cE", [128, 16]); sE = sbt(ph2, "sE", [128, 16]); dtmp = sbt(ph2, "dtmp", [128, 16, 256])
                MS('dve', cosA[:, :, 0:1], 1.0, ["cosA"]); MS('dve', sinA[:, :, 0:1], 0.0, ["sinA"])
                CP('dve', cE[:], cs1[:], ["cs1"], ["cE"]); CP('dve', sE[:], sn1[:], ["sn1"], ["sE"])
                m_ = 1
                while m_ <= 256:
                    bE = lambda t: _ap(t[:], 0, [list(t[:].ap[0]), [1, 16], [0, m_]])
                    co_o = cosA[:, :, 0:m_]; si_o = sinA[:, :, 0:m_]; co_n = cosA[:, :, m_:2 * m_]; si_n = sinA[:, :, m_:2 * m_]; tm = dtmp[:, :, 0:m_]
                    TT('dve', co_n, co_o, bE(cE), ALU.mult, ["cosA", "cE"], ["cosA"])
                    TT('dve', tm, si_o, bE(sE), ALU.mult, ["sinA", "sE"], ["dtmp"])
                    TT('dve', co_n, co_n, tm, ALU.subtract, ["cosA", "dtmp"], ["cosA"])
                    TT('dve', si_n, si_o, bE(cE), ALU.mult, ["sinA", "cE"], ["sinA"])
                    TT('dve', tm, co_o, bE(sE), ALU.mult, ["cosA", "sE"], ["dtmp"])
                    TT('dve', si_n, si_n, tm, ALU.add, ["sinA", "dtmp"], ["sinA"])
                    TT('dve', tmpa[:], cE[:], sE[:], ALU.mult, ["cE", "sE"], ["tmpa"])
                    TT('dve', tmpb[:], sE[:], sE[:], ALU.mult, ["sE"], ["tmpb"])
                    TT('dve', cE[:], cE[:], cE[:], ALU.mult, ["cE"], ["cE"])
                    TT('dve', cE[:], cE[:], tmpb[:], ALU.subtract, ["cE", "tmpb"], ["cE"])
                    TS('dve', sE[:], tmpa[:], 2.0, None, ALU.mult, None, ["tmpa"], ["sE"])
                    m_ *= 2
                CP('dve', c512[:], cE[:], ["cE"], ["c512"]); CP('dve', s512[:], sE[:], ["sE"], ["s512"])
                TS('dve', ns512[:], s512[:], -1.0, None, ALU.mult, None, ["s512"], ["ns512"])

                P.barrier()
            NB = 2
            wt = {n: [f2("%s_%d" % (n, i), [128, 512]) for i in range(NB)] for n in ("t1", "t2", "t3", "t4", "zr", "zi")}
            pt_ = {n: [f2("%s_%d" % (n, i), [128, 512], BF16) for i in range(NB)] for n in ("p1", "p2", "p3", "p4")}
            ini = [[f2("ini%s%d" % (a, i), [128, 16]) for i in range(2)] for a in "ri"]
            ut = [f2("ut%d" % i, [128, 4, 512], BF16) for i in range(2)]
            ypre = f2("ypre", [128, 512]); ysq = f2("ysq", [128, 512]); ysg = f2("ysg", [128, 512])
            y2f = f2("y2f", [128, 4, 512]); y2b = f2("y2b", [128, 4, 512], BF16); s5st = [f2("s5st%d" % i, [128, 4, 512], BF16) for i in range(2)]
            gsg = f2("gsg", [128, 512])
            for a in range(2):
                MS('dve', ini[a][0][:], 0.0, ["ini%d_0" % a])
            ucount = 0
            P.dma('sp', ut[0][:], hTv("uT")[:, :, 0:512], r=["uT"], w=["ut0"])
            for c in range(8):
                cb = c % 2; cs = slice(c * 512, (c + 1) * 512)
                if c + 1 < 8:
                    P.dma('sp', ut[1 - cb][:], hTv("uT")[:, :, (c + 1) * 512:(c + 2) * 512], r=["uT"], w=["ut%d" % (1 - cb)])
                for T in range(4):
                    Y, yk = psn(0, 2)
                    for pi0 in (0, 2):
                        units = []
                        for pi in (pi0, pi0 + 1):
                            pr = 4 * T + pi
                            b = ucount % NB; ucount += 1
                            A, ak = psn(2, 6); Bp, bk = psn(2, 6)
                            units.append(dict(pi=pi, pr=pr, b=b, A=A, ak=ak, Bp=Bp, bk=bk, co=cosA[:, pr, :], si=sinA[:, pr, :]))
                        kf = lambda u, n: "%s_%d" % (n, u["b"])

                        def st_mm(u):
                            MM(u["A"][:, :], BreT[:, u["pr"], :], ut[cb][:, T, :], True, True, ["BT0", "ut%d" % cb], [u["ak"]])
                            MM(u["Bp"][:, :], BimT[:, u["pr"], :], ut[cb][:, T, :], True, True, ["BT1", "ut%d" % cb], [u["bk"]])

                        def st_t1(u):
                            TT('dve', wt["t1"][u["b"]][:], u["A"][:, :], u["co"], ALU.mult, [u["ak"], "cosA"], [kf(u, "t1")])

                        def st_t2(u):
                            TT('dve', wt["t2"][u["b"]][:], u["Bp"][:, :], u["si"], ALU.mult, [u["bk"], "sinA"], [kf(u, "t2")])

                        def st_a1(u):
                            TT('dve', wt["t1"][u["b"]][:], wt["t1"][u["b"]][:], wt["t2"][u["b"]][:], ALU.add, [kf(u, "t1"), kf(u, "t2")], [kf(u, "t1")])

                        def st_t3(u):
                            TT('dve', wt["t3"][u["b"]][:], u["Bp"][:, :], u["co"], ALU.mult, [u["bk"], "cosA"], [kf(u, "t3")])

                        def st_t4(u):
                            TT('dve', wt["t4"][u["b"]][:], u["A"][:, :], u["si"], ALU.mult, [u["ak"], "sinA"], [kf(u, "t4")])

                        def st_a3(u):
                            TT('dve', wt["t3"][u["b"]][:], wt["t3"][u["b"]][:], wt["t4"][u["b"]][:], ALU.subtract, [kf(u, "t3"), kf(u, "t4")], [kf(u, "t3")])

                        def st_sr(u):
                            mg = mag[:, u["pr"]:u["pr"] + 1].to_broadcast([128, 512])
                            P.op('dve', 'tensor_tensor_scan', dict(out=wt["zr"][u["b"]][:], data0=mg, data1=wt["t1"][u["b"]][:], initial=ini[0][cb][:, u["pr"]:u["pr"] + 1], op0=ALU.mult, op1=ALU.add),
                                 ["mag", kf(u, "t1"), "ini0_%d" % cb], [kf(u, "zr")])

                        def st_si(u):
                            mg = mag[:, u["pr"]:u["pr"] + 1].to_broadcast([128, 512])
                            P.op('dve', 'tensor_tensor_scan', dict(out=wt["zi"][u["b"]][:], data0=mg, data1=wt["t3"][u["b"]][:], initial=ini[1][cb][:, u["pr"]:u["pr"] + 1], op0=ALU.mult, op1=ALU.add),
                                 ["mag", kf(u, "t3"), "ini1_%d" % cb], [kf(u, "zi")])

                        def st_carry(u):
                            if c >= 7:
                                return
                            nb_ = 1 - cb; pr = u["pr"]; bb = u["b"]
                            zrl = wt["zr"][bb][:, 511:512]; zil = wt["zi"][bb][:, 511:512]
                            ta = tmpa[:, 2 * bb:2 * bb + 1]; tb_ = tmpa[:, 2 * bb + 1:2 * bb + 2]; tk = "tmpa%d" % bb
                            TS('dve', ta, zrl, c512[:, pr:pr + 1], None, ALU.mult, None, [kf(u, "zr"), "c512"], [tk])
                            STT('dve', ini[0][nb_][:, pr:pr + 1], zil, ns512[:, pr:pr + 1], ta, ALU.mult, ALU.add, [kf(u, "zi"), "ns512", tk], ["ini0_%d" % nb_])
                            TS('dve', tb_, zrl, s512[:, pr:pr + 1], None, ALU.mult, None, [kf(u, "zr"), "s512"], [tk])
                            STT('dve', ini[1][nb_][:, pr:pr + 1], zil, c512[:, pr:pr + 1], tb_, ALU.mult, ALU.add, [kf(u, "zi"), "c512", tk], ["ini1_%d" % nb_])

                        def st_pool(u):
                            bb = u["b"]
                            TT('pool', pt_["p1"][bb][:], wt["zr"][bb][:], u["co"], ALU.mult, [kf(u, "zr"), "cosA"], [kf(u, "p1")])
                            TT('pool', pt_["p2"][bb][:], wt["zi"][bb][:], u["si"], ALU.mult, [kf(u, "zi"), "sinA"], [kf(u, "p2")])
                            TT('pool', pt_["p3"][bb][:], wt["zr"][bb][:], u["si"], ALU.mult, [kf(u, "zr"), "sinA"], [kf(u, "p3")])
                            TT('pool', pt_["p4"][bb][:], wt["zi"][bb][:], u["co"], ALU.mult, [kf(u, "zi"), "cosA"], [kf(u, "p4")])

                        def st_y(u):
                            bb = u["b"]; pi = u["pi"]; pr = u["pr"]
                            MM(Y[:, :], CWr[:, pr, :], pt_["p1"][bb][:], pi == 0, False, ["CW", kf(u, "p1")], [yk], sig=False)
                            MM(Y[:, :], NCr[:, pr, :], pt_["p2"][bb][:], False, False, ["CW", kf(u, "p2")], [yk], sig=False)
                            MM(Y[:, :], NCi[:, pr, :], pt_["p3"][bb][:], False, False, ["CW", kf(u, "p3")], [yk], sig=False)
                            MM(Y[:, :], NCi[:, pr, :], pt_["p4"][bb][:], False, pi == 3, ["CW", kf(u, "p4")], [yk], sig=True)
                        for stage in (st_mm, st_t1, st_t2, st_a1, st_t3, st_t4, st_a3, st_sr, st_si, st_carry, st_pool, st_y):
                            for u in units:
                                stage(u)
                    STT('dve', ypre[:], ut[cb][:, T, :], dsk[:, T:T + 1], Y[:, :], ALU.mult, ALU.add, ["ut%d" % cb, "dsk", yk], ["ypre"])
                    ACT(ysq[:], ypre[:], AF.Square, ["ypre"], ["ysq"])
                    TS('dve', ysq[:], ysq[:], 0.044715, 1.0, ALU.mult, ALU.add, ["ysq"], ["ysq"])
                    TT('dve', ysq[:], ysq[:], ypre[:], ALU.mult, ["ysq", "ypre"], ["ysq"])
                    ACT(ysg[:], ysq[:], AF.Sigmoid, ["ysq"], ["ysg"], scale=1.5957691216057308)
                    TT('dve', y2f[:, T, :], ypre[:], ysg[:], ALU.mult, ["ypre", "ysg"], ["y2f%d" % T])
                    CP('dve', y2b[:, T, :], y2f[:, T, :], ["y2f%d" % T], ["y2b"])
                sb_ = c % 2
                for ft in range(4):
                    G, gk = psn(2, 6)
                    for k in range(4):
                        MM(G[:, :], Wg[:, k, ft * 128:(ft + 1) * 128], y2b[:, k, :], k == 0, k == 3, [wgk, "y2b"], [gk], sig=(k == 3))
                    ACT(gsg[:], G[:, :], AF.Sigmoid, [gk, "bgl"], ["gsg"], bias=bgl[:, ft:ft + 1], scale=1.0)
                    TT('dve', s5st[sb_][:, ft, :], y2f[:, ft, :], gsg[:], ALU.mult, ["y2f%d" % ft, "gsg"], ["s5st%d" % sb_])
                P.dma('sp', hTv("s5T")[:, :, cs], s5st[sb_][:], r=["s5st%d" % sb_], w=["s5T"], acc=True)
            P.barrier()

    def outproj(srcA, srcB, wname, xin, xout, hTout, gname, goff):
        with contextlib.ExitStack() as ph:
            N_ = mk_norm(ph, gname, goff)
            W = sbt(ph, "Wo", [128, 8, 1024], BF16); wk = load_w(W, wname, 1024)
            cat = [sbt(ph, "cat%d" % i, [128, 8, 512], BF16) for i in range(2)]
            xt = [sbt(ph, "oxt%d" % i, [128, 1024]) for i in range(2)]
            hTs = [sbt(ph, "ohTs%d" % i, [128, 8, 512], BF16) for i in range(2)]
            def ld_cat(blk_):
                b_ = blk_ % 2; cs_ = slice(blk_ * 512, (blk_ + 1) * 512)
                P.dma('sp', cat[b_][:, 0:4, :], hTv(srcA)[:, :, cs_], r=[srcA], w=["cat%d" % b_], acc=True)
                P.dma('sp', cat[b_][:, 4:8, :], hTv(srcB)[:, :, cs_], r=[srcB], w=["cat%d" % b_], acc=True)

            def ld_x(tb_):
                P.dma('sp', xt[tb_ % 2][:], D[xin][tb_ * 128:(tb_ + 1) * 128, :], r=[xin], w=["oxt%d" % (tb_ % 2)])
            ld_cat(0); ld_x(0)
            pend = None

            def flush(pd):
                nb__, b__, tt__, blk__ = pd
                norm_B(N_, nb__, hTs[b__], "ohTs%d" % b__, tt__)
                if tt__ == 3:
                    P.dma('sp', hTv(hTout)[:, :, blk__ * 512:(blk__ + 1) * 512], hTs[b__][:], r=["ohTs%d" % b__], w=[hTout], acc=True)
            for blk in range(8):
                b = blk % 2; cs = slice(blk * 512, (blk + 1) * 512); ck = "cat%d" % b
                if blk + 1 < 8:
                    ld_cat(blk + 1)
                for tt in range(4):
                    tb = blk * 4 + tt; xb = tb % 2; xk = "oxt%d" % xb
                    if tb + 1 < 32:
                        ld_x(tb + 1)
                    ps0, k0 = psn(); ps1, k1 = psn()
                    for k in range(8):
                        lt = cat[b][:, k, tt * 128:(tt + 1) * 128]
                        MM(ps0[:, :], lt, W[:, k, 0:512], k == 0, k == 7, [wk, ck], [k0], sig=False)
                        MM(ps1[:, :], lt, W[:, k, 512:1024], k == 0, k == 7, [wk, ck], [k1], sig=(k == 7))
                    TT('dve', xt[xb][:, 0:512], xt[xb][:, 0:512], ps0[:, :], ALU.add, [xk, k0], [xk])
                    TT('dve', xt[xb][:, 512:1024], xt[xb][:, 512:1024], ps1[:, :], ALU.add, [xk, k1], [xk])
                    P.dma('sp', D[xout][tb * 128:(tb + 1) * 128, :], xt[xb][:], r=[xk], w=[xout], acc=True)
                    nb_ = norm_A(N_, xt[xb][:], xk)
                    if pend is not None:
                        flush(pend)
                    pend = (nb_, b, tt, blk)
            flush(pend)
            P.barrier()

    def prefetch_mlp(stk, wup, wdn):
        Wu = sbt(stk, "Wu", [128, 8, 4096], BF16); wuk = load_w(Wu, wup, 1024, 4)
        return (Wu, wuk)

    big0 = contextlib.ExitStack()
    pre0 = prefetch_mlp(big0, "wb_up0", "wb_dn0") if (5 in phases and 6 in phases) else None
    if 5 in phases:
        outproj("oT", "s5T", "wb_out0", "x", "xr1", "hT1", "g_mlp", 0)

    def mlp(wup, wdn, hTin, xin, xout, hTout, final=False, gname=None, goff=0, pre=None):
        with contextlib.ExitStack() as ph:
            N_ = mk_norm(ph, None if final else gname, goff)
            if pre is None:
                Wu = sbt(ph, "Wu", [128, 8, 4096], BF16); wuk = load_w(Wu, wup, 1024, 4)
                Wd = sbt(ph, "Wd", [128, 32, 1024], BF16); wdk = load_w(Wd, wdn, 4096, 4)
            else:
                Wu, wuk = pre
                Wd = sbt(ph, "Wd", [128, 32, 1024], BF16); wdk = load_w(Wd, wdn, 4096, 4)
            hb = [sbt(ph, "mhb%d" % i, [128, 8, 256], BF16) for i in range(2)]
            aT = sbt(ph, "aT", [128, 32, 256], BF16)
            rl = [sbt(ph, "rl%d" % i, [128, 256]) for i in range(2)]
            xt = [sbt(ph, "mxt%d" % i, [128, 1024]) for i in range(2)]
            if final:
                gfin = sbt(ph, "gfin", [128, 1024])
                P.dma('sp', gfin[:], _ap(D["g_final"], 0, [[0, 128], [1, 1024]]), w=["gfin"])
            else:
                hTs = [sbt(ph, "mhTs%d" % i, [128, 8, 256], BF16) for i in range(2)]
            def ld_x(tb_):
                P.dma('sp', xt[tb_ % 2][:], D[xin][tb_ * 128:(tb_ + 1) * 128, :], r=[xin], w=["mxt%d" % (tb_ % 2)])
            P.dma('sp', hb[0][:], hTv(hTin)[:, :, 0:256], r=[hTin], w=["mhb0"])
            ld_x(0)
            pend = [None]

            def mflush(pd):
                nb__, hbuf__, tb__ = pd
                norm_B(N_, nb__, hTs[hbuf__], "mhTs%d" % hbuf__, tb__ % 2)
                if tb__ % 2 == 1:
                    g2 = tb__ // 2
                    P.dma('sp', hTv(hTout)[:, :, g2 * 256:(g2 + 1) * 256], hTs[hbuf__][:], r=["mhTs%d" % hbuf__], w=[hTout], acc=True)
            for blk in range(16):
                b = blk % 2; cs = slice(blk * 256, (blk + 1) * 256); hk = "mhb%d" % b
                if blk + 1 < 16:
                    P.dma('sp', hb[1 - b][:], hTv(hTin)[:, :, (blk + 1) * 256:(blk + 2) * 256], r=[hTin], w=["mhb%d" % (1 - b)])
                for ft in range(32):
                    ps, pk = psn()
                    for k in range(8):
                        MM(ps[:, 0:256], Wu[:, k, ft * 128:(ft + 1) * 128], hb[b][:, k, :], k == 0, k == 7, [wuk, hk], [pk], sig=(k == 7))
                    rb = ft % 2
                    ACT(rl[rb][:], ps[:, 0:256], AF.Relu, [pk], ["rl%d" % rb])
                    TT('dve' if ft % 2 == 0 else 'pool', aT[:, ft, :], rl[rb][:], rl[rb][:], ALU.mult, ["rl%d" % rb], ["aT%d" % ft])
                akeys = ["aT%d" % ft for ft in range(32)]
                for tt in range(2):
                    tb = blk * 2 + tt; xb = tb % 2; xk = "mxt%d" % xb
                    if tb + 1 < 32:
                        ld_x(tb + 1)
                    ps0, k0 = psn(); ps1, k1 = psn()
                    for k in range(32):
                        lt = aT[:, k, tt * 128:(tt + 1) * 128]
                        MM(ps0[:, :], lt, Wd[:, k, 0:512], k == 0, k == 31, [wdk, "aT%d" % k], [k0], sig=False)
                        MM(ps1[:, :], lt, Wd[:, k, 512:1024], k == 0, k == 31, [wdk, "aT%d" % k], [k1], sig=(k == 31))
                    TT('dve', xt[xb][:, 0:512], xt[xb][:, 0:512], ps0[:, :], ALU.add, [xk, k0], [xk])
                    TT('dve', xt[xb][:, 512:1024], xt[xb][:, 512:1024], ps1[:, :], ALU.add, [xk, k1], [xk])
                    if not final:
                        hbuf = (tb // 2) % 2
                        P.dma('sp', D[xout][tb * 128:(tb + 1) * 128, :], xt[xb][:], r=[xk], w=[xout], acc=True)
                        nb_ = norm_A(N_, xt[xb][:], xk)
                        if pend[0] is not None:
                            mflush(pend[0])
                        pend[0] = (nb_, hbuf, tb)
                    else:
                        j = norm_stats(N_, xt[xb][:], xk)
                        TS('dve', xt[xb][:], xt[xb][:], N_["rs"][:, j:j + 1], None, ALU.mult, None, [xk, "nrs%d" % j], [xk])
                        TT('dve', xt[xb][:], xt[xb][:], gfin[:], ALU.mult, [xk, "gfin"], [xk])
                        P.dma('sp', D["out"][tb * 128:(tb + 1) * 128, :], xt[xb][:], r=[xk], w=["out"], acc=True)
            if not final and pend[0] is not None:
                mflush(pend[0])
            P.barrier()

    if 6 in phases:
        mlp("wb_up0", "wb_dn0", "hT1", "xr1", "xr2", "hT2", gname="g_odd", pre=pre0)
    big0.close()

    if 7 in phases:
        with contextlib.ExitStack() as ph:
            W = sbt(ph, "W7", [128, 8, 3584], BF16); wk = load_w(W, "wb_in1", 1024, 4)
            hb = [sbt(ph, "h7b%d" % i, [128, 8, 512], BF16) for i in range(2)]
            cw = sbt(ph, "cw", [128, 4, 3]); rc = sbt(ph, "rc", [128, 32, 64]); rsn = sbt(ph, "rsn", [128, 32, 64])
            qdec = sbt(ph, "qdec", [128, 4]); kdec = sbt(ph, "kdec", [128, 4])
            zb = [sbt(ph, "zb%d" % i, [128, 4, 514]) for i in range(2)]
            hcs = [sbt(ph, "hcs%d" % i, [128, 512]) for i in range(2)]
            acc = [sbt(ph, "cacc%d" % i, [128, 512]) for i in range(2)]
            cvst = [sbt(ph, "cvst%d" % i, [128, 4, 512], BF16) for i in range(2)]
            ra = [sbt(ph, "ra%d" % i, [128, 4, 64]) for i in range(2)]; rb = [sbt(ph, "rb%d" % i, [128, 4, 64]) for i in range(2)]
            qrot = [sbt(ph, "qrot%d" % i, [128, 4, 128]) for i in range(2)]
            sts = {n: [sbt(ph, "%sst%d" % (n, i), [128, 4, 512], BF16) for i in range(2)] for n in ("qd", "kd", "vt", "sg")}
            for j in range(3):
                P.dma('sp', cw[:, :, j], D["conv_w"][j].rearrange("(k p) -> p k", p=128), w=["cw"], acc=True)
            P.dma('sp', rc[:], D["c_rcos"].rearrange("(t p) i -> p t i", p=128), w=["rc"])
            P.dma('sp', rsn[:], D["c_rsin"].rearrange("(t p) i -> p t i", p=128), w=["rsn"])
            P.dma('sp', qdec[:], D["c_qdec"], w=["qdec"]); P.dma('sp', kdec[:], D["c_kdec"], w=["kdec"])
            MS('dve', zb[0][:, :, 0:2], 0.0, ["zb0h"])
            rcount = 0
            P.dma('sp', hb[0][:], hTv("hT2")[:, :, 0:512], r=["hT2"], w=["h7b0"])
            for blk in range(8):
                b = blk % 2; cs = slice(blk * 512, (blk + 1) * 512); hk = "h7b%d" % b
                if blk + 1 < 8:
                    P.dma('sp', hb[1 - b][:], hTv("hT2")[:, :, (blk + 1) * 512:(blk + 2) * 512], r=["hT2"], w=["h7b%d" % (1 - b)])

                def fm(col0):
                    ps, pk = psn()
                    for k in range(8):
                        MM(ps[:, :], W[:, k, col0:col0 + 128], hb[b][:, k, :], k == 0, k == 7, [wk, hk], [pk], sig=(k == 7))
                    return ps, pk
                for ft in range(4):
                    hbf = ft % 2
                    ps_h, kh = fm(ft * 128)
                    P.op('act', 'copy', dict(out=hcs[hbf][:], in_=ps_h[:, :]), [kh], ["hcs%d" % hbf])
                    ps_c, kc = fm(1024 + ft * 128)
                    zk = "zb%d_%d" % (b, ft)
                    TT('dve', zb[b][:, ft, 2:514], ps_c[:, :], hcs[hbf][:], ALU.mult, [kc, "hcs%d" % hbf], [zk])
                    hkey = "zb%dh" % b
                    TS('dve', acc[hbf][:], zb[b][:, ft, 2:514], cw[:, ft, 2:3], None, ALU.mult, None, [zk, "cw"], ["cacc%d" % hbf])
                    STT('dve', acc[hbf][:], zb[b][:, ft, 1:513], cw[:, ft, 1:2], acc[hbf][:], ALU.mult, ALU.add, [zk, hkey, "cw", "cacc%d" % hbf], ["cacc%d" % hbf])
                    STT('dve', acc[hbf][:], zb[b][:, ft, 0:512], cw[:, ft, 0:1], acc[hbf][:], ALU.mult, ALU.add, [zk, hkey, "cw", "cacc%d" % hbf], ["cacc%d" % hbf])
                    ps_b, kb = fm(512 + ft * 128)
                    TT('dve', cvst[b][:, ft, :], ps_b[:, :], acc[hbf][:], ALU.mult, [kb, "cacc%d" % hbf], ["cvst%d" % b])
                    CP('dve', zb[1 - b][:, ft, 0:2], zb[b][:, ft, 512:514], [zk, "zb%dh" % (1 - b)], ["zb%dh" % (1 - b)])
                P.dma('sp', hTv("convT")[:, :, cs], cvst[b][:], r=["cvst%d" % b], w=["convT"], acc=True)
                for tt in range(4):
                    tb = blk * 4 + tt

                    def tm(col0):
                        ps, pk = psn()
                        for k in range(8):
                            MM(ps[:, :], hb[b][:, k, tt * 128:(tt + 1) * 128], W[:, k, col0:col0 + 512], k == 0, k == 7, [wk, hk], [pk], sig=(k == 7))
                        return ps, pk
                    cosb = _ap(rc[:], tb * 64, [list(rc[:].ap[0]), [0, 4], [1, 64]])
                    sinb = _ap(rsn[:], tb * 64, [list(rsn[:].ap[0]), [0, 4], [1, 64]])
                    jobs = []
                    for nm, col0, dec in (("qd", 1536, qdec), ("kd", 2048, kdec)):
                        ps, pk = tm(col0)
                        r_ = rcount % 2; rcount += 1
                        pv = ps[:, :].rearrange("p (h d) -> p h d", h=4)
                        jobs.append(dict(nm=nm, dec=dec, pk=pk, r_=r_, x1=pv[:, :, 0:64], x2=pv[:, :, 64:128],
                                         ka="ra%d" % r_, kb_="rb%d" % r_, kq="qrot%d" % r_))
                    stages = [
                        lambda j: TT('dve', ra[j["r_"]][:], j["x1"], cosb, ALU.mult, [j["pk"], "rc"], [j["ka"]]),
                        lambda j: TT('dve', rb[j["r_"]][:], j["x2"], sinb, ALU.mult, [j["pk"], "rsn"], [j["kb_"]]),
                        lambda j: TT('dve', qrot[j["r_"]][:, :, 0:64], ra[j["r_"]][:], rb[j["r_"]][:], ALU.subtract, [j["ka"], j["kb_"]], [j["kq"]]),
                        lambda j: TT('dve', ra[j["r_"]][:], j["x1"], sinb, ALU.mult, [j["pk"], "rsn", j["kq"]], [j["ka"]]),
                        lambda j: TT('dve', rb[j["r_"]][:], j["x2"], cosb, ALU.mult, [j["pk"], "rc", j["kq"]], [j["kb_"]]),
                        lambda j: TT('dve', qrot[j["r_"]][:, :, 64:128], ra[j["r_"]][:], rb[j["r_"]][:], ALU.add, [j["ka"], j["kb_"]], [j["kq"]]),
                        lambda j: TT('dve', sts[j["nm"]][b][:, tt, :].rearrange("p (h d) -> p h d", h=4), qrot[j["r_"]][:],
                                     _ap(j["dec"][:], 0, [list(j["dec"][:].ap[0]), [1, 4], [0, 128]]), ALU.mult, [j["kq"], "qdec", "kdec"], ["%sst%d" % (j["nm"], b)]),
                    ]
                    for stg_ in stages:
                        for j in jobs:
                            stg_(j)
                    ps, pk = tm(2560)
                    evac(sts["vt"][b][:, tt, :], ps[:, :], [pk], ["vtst%d" % b])
                    ps, pk = tm(3072)
                    ACT(sts["sg"][b][:, tt, :], ps[:, :], AF.Silu, [pk], ["sgst%d" % b])
                for nm in ("qd", "kd", "vt", "sg"):
                    P.dma('sp', D[nm][cs, :].rearrange("(t p) c -> p t c", p=128), sts[nm][b][:], r=["%sst%d" % (nm, b)], w=[nm], acc=True)
            P.barrier()

    if 8 in phases:
        with contextlib.ExitStack() as ph:
            rmask = sbt(ph, "rmask", [128, 512]); stf = sbt(ph, "stf", [128, 4, 128]); stb = sbt(ph, "stb", [128, 4, 128], BF16)
            ld = {n: [sbt(ph, "%sl%d" % (n, i), [128, 4, 512], BF16) for i in range(2)] for n in ("qd", "kd", "vt", "sg")}
            qTs_ = [sbt(ph, "rqT%d" % i, [128, 512], BF16) for i in range(2)]; kTs_ = [sbt(ph, "rkT%d" % i, [128, 512], BF16) for i in range(2)]
            Pm = [sbt(ph, "Pm%d" % i, [128, 512], BF16) for i in range(2)]; on = [sbt(ph, "on%d" % i, [128, 512]) for i in range(2)]
            rg = [sbt(ph, "rg%d" % i, [128, 512], BF16) for i in range(2)]
            bs = sbt(ph, "bs", [128, 4, 6]); mv = sbt(ph, "mv", [128, 4, 2]); rstd = sbt(ph, "rstd", [128, 4]); sdv = sbt(ph, "sdv", [128, 4])
            retst = [sbt(ph, "retst%d" % i, [128, 4, 512], BF16) for i in range(2)]
            gam128 = [float((1.0 - 2.0 ** (-5.0 - h)) ** 128.0) for h in range(4)]
            P.dma('sp', rmask[:], D["c_rmask"], w=["rmask"])
            MS('dve', stf[:], 0.0, ["stf"]); MS('dve', stb[:], 0.0, ["stb"])
            def ld_grp(g_):
                for nm in ("qd", "kd", "vt", "sg"):
                    P.dma('sp', ld[nm][g_ % 2][:], D[nm][g_ * 512:(g_ + 1) * 512, :].rearrange("(t p) c -> p t c", p=128), r=[nm], w=["%sl%d" % (nm, g_ % 2)])
            ld_grp(0)
            for c in range(32):
                g4 = c // 4; b4 = g4 % 2; ci = c % 4; b = c % 2
                if ci == 0 and g4 + 1 < 8:
                    ld_grp(g4 + 1)
                lk = {nm: "%sl%d" % (nm, b4) for nm in ld}
                qd_c = ld["qd"][b4][:, ci, :]; kd_c = ld["kd"][b4][:, ci, :]; vt_c = ld["vt"][b4][:, ci, :]; sg_c = ld["sg"][b4][:, ci, :]
                hs = lambda h: slice(h * 128, (h + 1) * 128)
                pq, pqk = psbn()
                for h in range(4):
                    TR(pq[:, hs(h)], qd_c[:, hs(h)], identb[:], [lk["qd"], "identb"], [pqk], sig=(h == 3))
                P.op('act', 'copy', dict(out=qTs_[b][:], in_=pq[:, 0:512]), [pqk], ["rqT%d" % b])
                pk2, pkk = psbn()
                for h in range(4):
                    TR(pk2[:, hs(h)], kd_c[:, hs(h)], identb[:], [lk["kd"], "identb"], [pkk], sig=(h == 3))
                CP('dve', kTs_[b][:], pk2[:, 0:512], [pkk], ["rkT%d" % b])
                S, sk = psn()
                for h in range(4):
                    MM(S[:, hs(h)], kTs_[b][:, hs(h)], qTs_[b][:, hs(h)], True, True, ["rkT%d" % b, "rqT%d" % b], [sk], sig=(h == 3))
                TT('dve', Pm[b][:], S[:, :], rmask[:], ALU.mult, [sk, "rmask"], ["Pm%d" % b])
                O, ok = psn()
                for h in range(4):
                    MM(O[:, hs(h)], Pm[b][:, hs(h)], vt_c[:, hs(h)], True, False, ["Pm%d" % b, lk["vt"]], [ok], sig=False)
                    MM(O[:, hs(h)], qTs_[b][:, hs(h)], stb[:, h, :], False, True, ["rqT%d" % b, "stb"], [ok], sig=(h == 3))
                U, uk = psn()
                for h in range(4):
                    MM(U[:, hs(h)], kd_c[:, hs(h)], vt_c[:, hs(h)], True, True, [lk["kd"], lk["vt"]], [uk], sig=(h == 3))
                for h in range(4):
                    STT('dve', stf[:, h, :], stf[:, h, :], gam128[h], U[:, hs(h)], ALU.mult, ALU.add, ["stf", uk], ["stf"])
                CP('dve', stb[:], stf[:], ["stf"], ["stb"])
                for h in range(4):
                    P.op('dve', 'bn_stats', dict(out=bs[:, h, :], in_=O[:, hs(h)]), [ok], ["bs"])
                    P.op('dve', 'bn_aggr', dict(out=mv[:, h, :], in_=bs[:, h, :]), ["bs"], ["mv"])
                ACT(sdv[:], mv[:, :, 1], AF.Sqrt, ["mv"], ["sdv"], bias=EPS, scale=1.0)
                P.op('dve', 'reciprocal', dict(out=rstd[:], in_=sdv[:]), ["sdv"], ["rstd"])
                for h in range(4):
                    TS('dve', on[b][:, hs(h)], O[:, hs(h)], mv[:, h, 0:1], rstd[:, h:h + 1], ALU.subtract, ALU.mult, [ok, "mv", "rstd"], ["on%d" % b])
                TT('dve', rg[b][:], on[b][:], sg_c, ALU.mult, ["on%d" % b, lk["sg"]], ["rg%d" % b])
                pr_, prk = psbn()
                for h in range(4):
                    TR(pr_[:, hs(h)], rg[b][:, hs(h)], identb[:], ["rg%d" % b, "identb"], [prk], sig=(h == 3))
                P.op('act', 'copy', dict(out=retst[b4][:, :, ci * 128:(ci + 1) * 128], in_=pr_[:, 0:512].rearrange("p (h t) -> p h t", h=4)), [prk], ["retst%d" % b4])
                if ci == 3:
                    P.dma('sp', hTv("retT")[:, :, g4 * 512:(g4 + 1) * 512], retst[b4][:], r=["retst%d" % b4], w=["retT"], acc=True)
            P.barrier()

    big1 = contextlib.ExitStack()
    pre1 = prefetch_mlp(big1, "wb_up1", "wb_dn1") if (9 in phases and 10 in phases) else None
    if 9 in phases:
        outproj("convT", "retT", "wb_out1", "xr2", "xr3", "hT3", "g_mlp", 1024)
    if 10 in phases:
        mlp("wb_up1", "wb_dn1", "hT3", "xr3", None, None, final=True, pre=pre1)
    big1.close()

    P.barrier()
    P.emit()
    st.close()
    return nc


def _consts():
    bf = ml_dtypes.bfloat16
    c = {}
    c["c_identb"] = np.eye(128, dtype=np.float32).astype(bf)
    c["c_identf"] = np.eye(128, dtype=np.float32)
    s = np.arange(128)
    c["c_trib"] = (s[None, :] >= s[:, None]).astype(np.float32).astype(bf)
    inv = (1.0 / (np.float32(10000.0) ** np.linspace(0.0, 1.0, 64, dtype=np.float32))).astype(np.float32)
    ang = (np.arange(NT, dtype=np.float32)[:, None] * inv[None, :]).astype(np.float32)
    c["c_rcos"] = np.cos(ang).astype(np.float32)
    c["c_rsin"] = np.sin(ang).astype(np.float32)
    gam = (1.0 - 2.0 ** (-5.0 - np.arange(4, dtype=np.float64)))
    idx = np.arange(128, dtype=np.float64)
    c["c_qdec"] = (gam[None, :] ** (idx[:, None] + 1.0)).astype(np.float32)
    c["c_kdec"] = ((gam[None, :] ** (127.0 - idx[:, None])) * (128.0 ** -0.5)).astype(np.float32)
    tri = (s[None, :] >= s[:, None]).astype(np.float64)
    c["c_rmask"] = np.concatenate([tri * gam[h] ** (-128.0) for h in range(4)], axis=1).astype(np.float32)
    c["c_iota"] = np.tile(np.arange(512, dtype=np.float32)[None, :], (128, 1))
    m = np.arange(128)
    c["c_cmask"] = ((m[None, :] // 64) == ((m[:, None] // 16) % 2)).astype(np.float32)
    return c


def _perm_w_in1(w):
    idx = list(range(1536))
    for base in (1536, 2048):
        for h in range(4):
            idx += [base + h * 128 + 2 * i for i in range(64)] + [base + h * 128 + 2 * i + 1 for i in range(64)]
    idx += list(range(2560, 3584))
    return np.ascontiguousarray(w[:, idx])


def _host_inputs(inp):
    f = lambda a: np.ascontiguousarray(np.asarray(a, dtype=np.float32))
    sh = {
        "g_even": f(inp["even_norm_mix"])[0], "w_in0": f(inp["even_w_in"])[0], "b_forget": f(inp["even_b_forget"])[0],
        "log_dt": f(inp["even_s5_log_dt"])[0], "lam_re": f(inp["even_s5_lambda_re"])[0], "lam_im": f(inp["even_s5_lambda_im"])[0],
        "b_re": f(inp["even_s5_b_re"])[0], "b_im": f(inp["even_s5_b_im"])[0], "c_re": f(inp["even_s5_c_re"])[0], "c_im": f(inp["even_s5_c_im"])[0],
        "d_skip": f(inp["even_s5_d"])[0], "w_glu": f(inp["even_s5_w_glu"])[0], "b_glu": f(inp["even_s5_b_glu"])[0], "w_out0": f(inp["even_w_out"])[0],
        "g_odd": f(inp["odd_norm_mix"])[0], "w_in1": _perm_w_in1(f(inp["odd_w_in"])[0]), "conv_w": f(inp["odd_conv_w"])[0], "w_out1": f(inp["odd_w_out"])[0],
        "g_mlp": f(inp["mlp_norm"]), "w_up": f(inp["mlp_w_up"]), "w_down": f(inp["mlp_w_down"]), "g_final": f(inp["final_norm"]),
    }
    sh.update(_consts())
    return sh


_NC_CACHE = {}
FUSED = True
_GROUPS = [((1, 2, 3), (), ("oT", "uT")), ((4,), ("uT",), ("s5T",)), ((5, 6), ("oT", "s5T"), ("xr2", "hT2")),
           ((7, 8), ("hT2",), ("convT", "retT")), ((9, 10), ("convT", "retT", "xr2"), ())]


def kernel(**inputs):
    x = np.ascontiguousarray(np.asarray(inputs["x"], dtype=np.float32))
    shared = _host_inputs(inputs)
    if FUSED:
        if "nc" not in _NC_CACHE:
            _NC_CACHE["nc"] = build()
        nc = _NC_CACHE["nc"]
        in_maps = [dict(shared, x=x[b]) for b in range(8)]
        res = run_bass_kernel_spmd(nc, in_maps, core_ids=list(range(8)))
        return np.stack([np.asarray(r["out"], dtype=np.float32) for r in res.results], axis=0)
    carry = [dict() for _ in range(8)]
    res = None
    for gi, (phs, feed, outs) in enumerate(_GROUPS):
        key = "g%d" % gi
        if key not in _NC_CACHE:
            _NC_CACHE[key] = build(phases=phs, dbg=outs, feed=feed)
        nc = _NC_CACHE[key]
        in_maps = [dict(shared, x=x[b], **{n: carry[b][n] for n in feed}) for b in range(8)]
        res = run_bass_kernel_spmd(nc, in_maps, core_ids=list(range(8)))
        for b in range(8):
            for n in outs:
                carry[b][n] = np.ascontiguousarray(np.asarray(res.results[b][n]))
    return np.stack([np.asarray(r["out"], dtype=np.float32) for r in res.results], axis=0)
```

```python
import contextlib
import numpy as np
import concourse.bass as bass
import concourse.mybir as mybir

F32 = mybir.dt.float32
BF16 = mybir.dt.bfloat16
ALU = mybir.AluOpType
AF = mybir.ActivationFunctionType
AX = mybir.AxisListType

SAME_ENGINE_SYNC = True


class Prog:
    ENG = ['pe', 'act', 'dve', 'pool', 'sp']
    MAIN = ['pe', 'act', 'dve', 'sp']

    def __init__(self, nc, stack, nds_main=24, nds_pool=6):
        self.nc = nc
        self.eng = {'pe': nc.tensor, 'act': nc.scalar, 'dve': nc.vector,
                    'pool': nc.gpsimd, 'sp': nc.sync}
        self.ops = {e: [] for e in self.ENG}
        self.sem = {e: stack.enter_context(nc.semaphore('s_' + e)) for e in self.ENG}
        nds = nds_main + nds_pool
        self.dsem = [stack.enter_context(nc.semaphore('d%d' % i)) for i in range(nds)]
        self.dcnt = [0] * nds
        self.grp = {'main': list(range(nds_main)), 'pool': list(range(nds_main, nds))}
        self.gnext = {'main': 0, 'pool': 0}
        self.res = {}

    def _deps(self, reads, writes):
        deps = set()
        for k in reads:
            r = self.res.get(k)
            if r:
                deps.update(r[0])
        for k in writes:
            r = self.res.get(k)
            if r:
                deps.update(r[0])
                for e, p in r[1].items():
                    deps.add(('e', e, p))
                deps.update(r[2])
        return deps

    def _mark(self, tok, reads, writes, acc=False):
        for k in reads:
            r = self.res.setdefault(k, [[], {}, []])
            if tok[0] == 'e':
                r[1][tok[1]] = tok[2]
            else:
                r[2].append(tok)
        for k in writes:
            if acc and k in self.res:
                self.res[k][0] = [t for t in self.res[k][0] if not (t[0] == 'd' and tok[0] == 'd' and t[1] == tok[1])] + [tok]
            else:
                self.res[k] = [[tok], {}, []]

    def op(self, e, fn, kw=None, r=(), w=(), sig=True):
        if isinstance(fn, str):
            meth, kws = fn, dict(kw)
            fn = lambda eng: getattr(eng, meth)(**kws)
        deps = self._deps(r, w)
        pos = len(self.ops[e])
        tok = ('e', e, pos)
        self.ops[e].append(dict(fn=fn, deps=deps, sig=sig, dma=None))
        self._mark(tok, r, w)
        return tok

    def dma(self, q, out, in_, r=(), w=(), grp=None, acc=False, **kw):
        grp = grp or ('pool' if q == 'pool' else 'main')
        lst = self.grp[grp]
        i = lst[self.gnext[grp] % len(lst)]
        self.gnext[grp] += 1
        deps = self._deps(r, w)
        if self.dcnt[i] > 0:
            deps.add(('d', i, self.dcnt[i]))
        self.dcnt[i] += 16
        tok = ('d', i, self.dcnt[i])
        self.ops[q].append(dict(fn=lambda eng: eng.dma_start(out=out, in_=in_, **kw),
                                deps=deps, sig=False, dma=i))
        self._mark(tok, r, w, acc)
        return tok

    def barrier(self, engs=None):
        engs = engs or self.MAIN
        toks = set()
        for e in engs:
            for p in range(len(self.ops[e]) - 1, -1, -1):
                o = self.ops[e][p]
                if o['sig'] and o['dma'] is None and o['fn'] is not None:
                    toks.add(('e', e, p))
                    break
        for i in self.grp['main']:
            if self.dcnt[i] > 0:
                toks.add(('d', i, self.dcnt[i]))
        for e in engs:
            self.ops[e].append(dict(fn=None, deps=set(toks), sig=False, dma=None))

    def emit(self):
        nc = self.nc
        cum = {}
        for e in self.ENG:
            c = 0
            arr = []
            for o in self.ops[e]:
                if o['sig'] and o['dma'] is None:
                    c += 1
                arr.append(c)
            need = [0] * len(arr)
            nxt = None
            for p in range(len(arr) - 1, -1, -1):
                o = self.ops[e][p]
                if o['sig'] and o['dma'] is None:
                    nxt = arr[p]
                need[p] = nxt
            cum[e] = need
        self.total = {e: (cum[e] and max([x for x in cum[e] if x is not None] or [0])) for e in self.ENG}

        def resolve(tok):
            if tok[0] == 'd':
                return self.dsem[tok[1]], ('d', tok[1]), tok[2]
            _, e, p = tok
            v = cum[e][p]
            assert v is not None, ("no signalling op after", tok)
            return self.sem[e], ('e', e), v

        with nc.Block() as block:
            for e in self.ENG:
                ops = self.ops[e]
                if not ops:
                    continue

                def body(engine, e=e, ops=ops):
                    seen = {}
                    for pos, o in enumerate(ops):
                        for tok in sorted(o['deps']):
                            if tok[0] == 'e' and tok[1] == e:
                                if e in ('pe', 'sp') or (not SAME_ENGINE_SYNC and o['dma'] is None):
                                    continue
                                if tok[2] >= pos:
                                    continue
                            sem, key, val = resolve(tok)
                            if seen.get(key, 0) >= val:
                                continue
                            engine.wait_ge(sem, val)
                            seen[key] = val
                        if o['fn'] is None:
                            continue
                        ins = o['fn'](engine)
                        if o['dma'] is not None:
                            ins.then_inc(self.dsem[o['dma']], 16)
                        elif o['sig']:
                            ins.then_inc(self.sem[e], 1)

                getattr(block, {'pe': 'tensor', 'act': 'scalar', 'dve': 'vector',
                                'pool': 'gpsimd', 'sp': 'sync'}[e])(body)

from concourse.bass_utils import run_bass_kernel_spmd
import ml_dtypes
import math

NT = 4096
EPS = 1e-6
PI = math.pi


def _ap(t, off, pat):
    return bass.AP(t.tensor, t.offset + off, [list(p) for p in pat])


def build(phases=None, dbg=(), feed=()):
    phases = set(range(1, 11)) if phases is None else set(phases)
    nc = bass.Bass("TRN2", target_bir_lowering=False)
    D = {}

    def din(n, shape, dt=F32):
        D[n] = nc.dram_tensor(n, list(shape), dt, kind="ExternalInput").ap()

    def dsc(n, shape, dt=BF16):
        kind = "ExternalInput" if n in feed else ("ExternalOutput" if n in dbg else "Internal")
        D[n] = nc.dram_tensor(n, list(shape), dt, kind=kind).ap()

    din("x", [NT, 1024]); din("g_even", [1024]); din("w_in0", [1024, 2056]); din("b_forget", [8])
    din("log_dt", [32]); din("lam_re", [32, 64]); din("lam_im", [32, 64])
    din("b_re", [32, 64, 16]); din("b_im", [32, 64, 16]); din("c_re", [32, 16, 64]); din("c_im", [32, 16, 64])
    din("d_skip", [512]); din("w_glu", [512, 512]); din("b_glu", [512]); din("w_out0", [1024, 1024])
    din("g_odd", [1024]); din("w_in1", [1024, 3584]); din("conv_w", [3, 512]); din("w_out1", [1024, 1024])
    din("g_mlp", [2, 1024]); din("w_up", [2, 1024, 4096]); din("w_down", [2, 4096, 1024]); din("g_final", [1024])
    din("c_identb", [128, 128], BF16); din("c_identf", [128, 128]); din("c_trib", [128, 128], BF16)
    din("c_rcos", [NT, 64]); din("c_rsin", [NT, 64]); din("c_qdec", [128, 4]); din("c_kdec", [128, 4])
    din("c_rmask", [128, 512]); din("c_iota", [128, 512]); din("c_cmask", [128, 128])
    D["out"] = nc.dram_tensor("out", [NT, 1024], F32, kind="ExternalOutput").ap()
    for n, sh in [("wb_in0", [1024, 2056]), ("wb_glu", [512, 512]), ("wb_out0", [1024, 1024]), ("wb_up0", [1024, 4096]),
                  ("wb_dn0", [4096, 1024]), ("wb_in1", [1024, 3584]), ("wb_out1", [1024, 1024]), ("wb_up1", [1024, 4096]),
                  ("wb_dn1", [4096, 1024]), ("hT0", [1024, NT]), ("hT1", [1024, NT]), ("hT2", [1024, NT]), ("hT3", [1024, NT]),
                  ("qT", [512, NT]), ("kT", [512, NT]), ("uT", [512, NT]), ("vaug", [NT, 1024]),
                  ("oT", [512, NT]), ("s5T", [512, NT]), ("convT", [512, NT]), ("retT", [512, NT]),
                  ("qd", [NT, 512]), ("kd", [NT, 512]), ("vt", [NT, 512]), ("sg", [NT, 512])]:
        dsc(n, sh)
    dsc("cT", [8, NT], F32); dsc("xr1", [NT, 1024], F32); dsc("xr2", [NT, 1024], F32); dsc("xr3", [NT, 1024], F32)

    st = contextlib.ExitStack()
    ncd = st.enter_context(nc.allow_non_contiguous_dma(reason="small param loads"))
    P = Prog(nc, st)
    _uid = [0]

    def sbt(stk, name, shape, dt=F32):
        _uid[0] += 1
        return stk.enter_context(nc.sbuf_tensor("%s_u%d" % (name, _uid[0]), list(shape), dt))

    def MM(out, lhsT, rhs, start, stop, r, w, sig=True):
        P.op('pe', 'matmul', dict(out=out, lhsT=lhsT, rhs=rhs, start=start, stop=stop), r, w, sig)

    def TR(out, in_, ident, r, w, sig=True):
        P.op('pe', 'transpose', dict(out=out, in_=in_, identity=ident), r, w, sig)

    def ACT(out, in_, func, r, w, **kw):
        P.op('act', 'activation', dict(out=out, in_=in_, func=func, **kw), r, w)

    def TT(eng, out, in0, in1, op, r, w):
        P.op(eng, 'tensor_tensor', dict(out=out, in0=in0, in1=in1, op=op), r, w)

    def TS(eng, out, in0, s1, s2, op0, op1, r, w):
        kw = dict(out=out, in0=in0, scalar1=s1, scalar2=s2, op0=op0)
        if op1 is not None:
            kw['op1'] = op1
        P.op(eng, 'tensor_scalar', kw, r, w)

    def STT(eng, out, in0, scalar, in1, op0, op1, r, w):
        P.op(eng, 'scalar_tensor_tensor', dict(out=out, in0=in0, scalar=scalar, in1=in1, op0=op0, op1=op1), r, w)

    def CP(eng, out, in_, r, w):
        P.op(eng, 'tensor_copy', dict(out=out, in_=in_), r, w)

    def MS(eng, ap, val, w):
        P.op(eng, 'memset', dict(ap=ap, constant=val), (), w)

    identb = sbt(st, "identb", [128, 128], BF16); identf = sbt(st, "identf", [128, 128]); trib = sbt(st, "trib", [128, 128], BF16)
    gains = sbt(st, "gains", [128, 4, 8])
    stg = [sbt(st, "stg%d" % i, [128, 2048]) for i in range(2)]
    wo = [sbt(st, "wo%d" % i, [128, 2048], BF16) for i in range(2)]
    psf = [st.enter_context(nc.psum_tensor("psf%d" % i, [128, 512], F32)) for i in range(6)]
    psb = [st.enter_context(nc.psum_tensor("psb%d" % i, [128, 1024], BF16)) for i in range(2)]
    cnt = {"psf": 0, "psb": 0, "nrm": 0, "ev": 0}

    def psn(lo=0, hi=6):
        i = lo + cnt["psf"] % (hi - lo); cnt["psf"] += 1
        return psf[i], "psf%d" % i

    def psbn():
        i = cnt["psb"] % 2; cnt["psb"] += 1
        return psb[i], "psb%d" % i

    P.dma('sp', identb[:], D["c_identb"], w=["identb"]); P.dma('sp', identf[:], D["c_identf"], w=["identf"])
    P.dma('sp', trib[:], D["c_trib"], w=["trib"])
    for wi, src in enumerate([D["g_even"], D["g_mlp"][0], D["g_odd"], D["g_mlp"][1]]):
        P.dma('pool', gains[:, wi, :], src.rearrange("(k p) -> p k", p=128), w=["gains"], acc=True)

    wpc = [0]

    def wprep(src, dstn, K, N, gi=None):
        dst = D[dstn]
        for k in range(K // 128):
            for c0 in range(0, N, 2048):
                n = min(N, c0 + 2048) - c0
                b = wpc[0] % 2; wpc[0] += 1
                P.dma('pool', stg[b][:, :n], src[k * 128:(k + 1) * 128, c0:c0 + n], w=["stg%d" % b])
                if gi is None:
                    CP('pool', wo[b][:, :n], stg[b][:, :n], ["stg%d" % b], ["wo%d" % b])
                else:
                    TS('pool', wo[b][:, :n], stg[b][:, :n], gains[:, gi, k:k + 1], None, ALU.mult, None, ["stg%d" % b, "gains"], ["wo%d" % b])
                P.dma('pool', dst[k * 128:(k + 1) * 128, c0:c0 + n], wo[b][:, :n], r=["wo%d" % b], w=[dstn], acc=True)

    if 2 in phases: wprep(D["w_in0"], "wb_in0", 1024, 2056)
    if 4 in phases: wprep(D["w_glu"], "wb_glu", 512, 512)
    if 5 in phases: wprep(D["w_out0"], "wb_out0", 1024, 1024)
    if 6 in phases:
        wprep(D["w_up"][0], "wb_up0", 1024, 4096)
        wprep(D["w_down"][0], "wb_dn0", 4096, 1024)
    if 7 in phases: wprep(D["w_in1"], "wb_in1", 1024, 3584)
    if 9 in phases: wprep(D["w_out1"], "wb_out1", 1024, 1024)
    if 10 in phases:
        wprep(D["w_up"][1], "wb_up1", 1024, 4096)
        wprep(D["w_down"][1], "wb_dn1", 4096, 1024)

    def load_w(tile, name, K, nsplit=2):
        src = D[name].rearrange("(k p) n -> p k n", p=128)
        kk = K // 128
        step = max(1, kk // nsplit)
        for k0 in range(0, kk, step):
            P.dma('sp', tile[:, k0:k0 + step, :], src[:, k0:k0 + step, :], r=[name], w=["W_" + name], acc=True)
        return "W_" + name

    def mk_norm(stk, gname=None, goff=0):
        gv = None
        if gname is not None:
            gv = sbt(stk, "ngv", [128, 1024])
            P.dma('sp', gv[:], _ap(D[gname], goff, [[0, 128], [1, 1024]]), w=["ngv"])
        return dict(gv=gv, junk=sbt(stk, "njunk", [128, 1024], BF16), ss=sbt(stk, "nss", [128, 64]), sd=sbt(stk, "nsd", [128, 64]),
                    rs=sbt(stk, "nrs", [128, 64]), hb=[sbt(stk, "nhb%d" % i, [128, 1024], BF16) for i in range(2)])

    def norm_stats(N_, xt, xkey):
        j = cnt["nrm"] % 64; cnt["nrm"] += 1
        ACT(N_["junk"][:], xt, AF.Square, [xkey], ["njunk", "nss%d" % j], accum_out=N_["ss"][:, j:j + 1])
        ACT(N_["sd"][:, j:j + 1], N_["ss"][:, j:j + 1], AF.Sqrt, ["nss%d" % j], ["nsd%d" % j], bias=EPS, scale=1.0 / 1024)
        P.op('dve', 'reciprocal', dict(out=N_["rs"][:, j:j + 1], in_=N_["sd"][:, j:j + 1]), ["nsd%d" % j], ["nrs%d" % j])
        return j

    def norm_A(N_, xt, xkey):
        j = norm_stats(N_, xt, xkey)
        b = j % 2
        STT('dve', N_["hb"][b][:], xt, N_["rs"][:, j:j + 1], N_["gv"][:], ALU.mult, ALU.mult, [xkey, "nrs%d" % j, "ngv"], ["nhb%d" % b])
        return b

    def norm_B(N_, b, hTs, hkey, col):
        pt, pk = psbn()
        for k in range(8):
            TR(pt[:, k * 128:(k + 1) * 128], N_["hb"][b][:, k * 128:(k + 1) * 128], identb[:], ["nhb%d" % b, "identb"], [pk], sig=(k == 7))
        P.op('act', 'copy', dict(out=hTs[:, :, col * 128:(col + 1) * 128], in_=pt[:].rearrange("p (k t) -> p k t", k=8)), [pk], [hkey])

    def norm_T(N_, xt, xkey, hTs, hkey, col):
        b = norm_A(N_, xt, xkey)
        norm_B(N_, b, hTs, hkey, col)

    def evac(out, in_, r, w):
        cnt["ev"] += 1
        if cnt["ev"] % 2 == 0:
            P.op('act', 'copy', dict(out=out, in_=in_), r, w)
        else:
            CP('dve', out, in_, r, w)

    hTv = lambda n: D[n].rearrange("(k p) t -> p k t", p=128)

    if 1 in phases:
        with contextlib.ExitStack() as ph:
            N_ = mk_norm(ph, "g_even")
            xt = [sbt(ph, "xt%d" % i, [128, 1024]) for i in range(2)]
            hTs = [sbt(ph, "hTs%d" % i, [128, 8, 512], BF16) for i in range(2)]
            P.dma('sp', xt[0][:], D["x"][0:128, :], w=["xt0"])
            for tb in range(32):
                b = tb % 2; hb_ = (tb // 4) % 2
                if tb + 1 < 32:
                    P.dma('sp', xt[1 - b][:], D["x"][(tb + 1) * 128:(tb + 2) * 128, :], w=["xt%d" % (1 - b)])
                norm_T(N_, xt[b][:], "xt%d" % b, hTs[hb_], "hTs%d" % hb_, tb % 4)
                if tb % 4 == 3:
                    blk = tb // 4
                    P.dma('sp', hTv("hT0")[:, :, blk * 512:(blk + 1) * 512], hTs[hb_][:], r=["hTs%d" % hb_], w=["hT0"], acc=True)
            P.barrier()

    if 2 in phases:
        with contextlib.ExitStack() as ph:
            W = sbt(ph, "W2", [128, 8, 2056], BF16); wk = load_w(W, "wb_in0", 1024)
            hb = [sbt(ph, "hblk%d" % i, [128, 8, 512], BF16) for i in range(2)]
            ost = [sbt(ph, "ost%d" % i, [128, 12, 512], BF16) for i in range(2)]
            vst = [sbt(ph, "vst%d" % i, [128, 4, 8, 128], BF16) for i in range(2)]
            negb = sbt(ph, "negb", [8, 1]); fsp = sbt(ph, "fsp", [8, NT]); ftmp = sbt(ph, "ftmp", [8, 512]); one8 = sbt(ph, "one8", [8, 1])
            cTs = sbt(ph, "cTs", [8, NT])
            P.dma('sp', negb[:], D["b_forget"].rearrange("(h o) -> h o", o=1), w=["negb"])
            TS('dve', negb[:], negb[:], -1.0, None, ALU.mult, None, ["negb"], ["negb"])
            MS('dve', one8[:], 1.0, ["one8"])
            for i in range(2):
                MS('dve', vst[i][:], 1.0, ["vst%d" % i])
            fcols = [0, 128, 256, 384, 512, 640, 768, 896, 1544, 1672, 1800, 1928]
            P.dma('sp', hb[0][:], hTv("hT0")[:, :, 0:512], r=["hT0"], w=["hblk0"])
            for blk in range(8):
                b = blk % 2; hk = "hblk%d" % b; cs = slice(blk * 512, (blk + 1) * 512)
                if blk + 1 < 8:
                    P.dma('sp', hb[1 - b][:], hTv("hT0")[:, :, (blk + 1) * 512:(blk + 2) * 512], r=["hT0"], w=["hblk%d" % (1 - b)])
                for ft in range(12):
                    ps, pk = psn()
                    for k in range(8):
                        MM(ps[:, :], W[:, k, fcols[ft]:fcols[ft] + 128], hb[b][:, k, :], k == 0, k == 7, [wk, hk], [pk], sig=(k == 7))
                    evac(ost[b][:, ft, :], ps[:, :], [pk], ["ost%d_%d" % (b, ft // 4)])
                for gi, nm in enumerate(["qT", "kT", "uT"]):
                    P.dma('sp', hTv(nm)[:, :, cs], ost[b][:, gi * 4:(gi + 1) * 4, :], r=["ost%d_%d" % (b, gi)], w=[nm], acc=True)
                ps, pk = psn()
                for k in range(8):
                    MM(ps[:, :], W[:, k, 1536:1664], hb[b][:, k, :], k == 0, k == 7, [wk, hk], [pk], sig=(k == 7))
                ACT(ftmp[:], ps[0:8, :], AF.Exp, [pk, "negb"], ["ftmp"], bias=negb[:, 0:1], scale=-1.0)
                ACT(fsp[:, cs], ftmp[:], AF.Ln, ["ftmp"], ["fsp"], bias=1.0, scale=1.0)
                for tt in range(4):
                    ps, pk = psn()
                    for k in range(8):
                        MM(ps[:, :], hb[b][:, k, tt * 128:(tt + 1) * 128], W[:, k, 1024:1536], k == 0, k == 7, [wk, hk], [pk], sig=(k == 7))
                    evac(vst[b][:, tt, :, 0:64], ps[:, :].rearrange("p (h d) -> p h d", h=8), [pk], ["vst%d" % b])
                P.dma('sp', D["vaug"][cs, :].rearrange("(t p) c -> p t c", p=128), vst[b][:].rearrange("p t h d -> p t (h d)"), r=["vst%d" % b], w=["vaug"], acc=True)
            P.op('dve', 'tensor_tensor_scan', dict(out=cTs[:], data0=one8[:, 0:1].to_broadcast([8, NT]), data1=fsp[:], initial=0.0, op0=ALU.mult, op1=ALU.subtract),
                 ["one8", "fsp"], ["cTs"])
            P.dma('sp', D["cT"], cTs[:], r=["cTs"], w=["cT"])
            P.barrier()

    if 3 in phases:
        with contextlib.ExitStack() as ph:
            qTs = sbt(ph, "qTs", [128, 4, NT], BF16); kTs = sbt(ph, "kTs", [128, 4, NT], BF16)
            vs = sbt(ph, "vs", [128, 32, 1024], BF16)
            ccol = sbt(ph, "ccol", [128, 32, 8]); R = sbt(ph, "R", [128, 8, 8]); biasT = sbt(ph, "biasT", [128, 32, 8, 8])
            Pt = [sbt(ph, "Pt%d" % i, [128, 512], BF16) for i in range(4)]
            qm = [sbt(ph, "qm%d" % i, [128, 512], BF16) for i in range(2)]
            rd = sbt(ph, "rd", [128, 512]); fost = [sbt(ph, "fost%d" % i, [128, NT], BF16) for i in range(2)]
            for k in range(4):
                P.dma('sp', qTs[:, k, :], D["qT"][k * 128:(k + 1) * 128, :], r=["qT"], w=["qTs"], acc=True)
                P.dma('sp', kTs[:, k, :], D["kT"][k * 128:(k + 1) * 128, :], r=["kT"], w=["kTs"], acc=True)
            vv = D["vaug"].rearrange("(t p) c -> p t c", p=128)
            for t0 in range(0, 32, 8):
                P.dma('sp', vs[:, t0:t0 + 8, :], vv[:, t0:t0 + 8, :], r=["vaug"], w=["vs"], acc=True)
            for hh in range(8):
                P.dma('sp', ccol[:, :, hh], D["cT"][hh].rearrange("(j p) -> p j", p=128), r=["cT"], w=["ccol"], acc=True)
                P.dma('sp', R[:, :, hh], _ap(D["cT"], hh * NT, [[0, 128], [512, 8]]), r=["cT"], w=["R"], acc=True)
            for i in range(32):
                TT('dve', biasT[:, i, :, :], R[:], _ap(ccol[:], i * 8, [list(ccol[:].ap[0]), [0, 8], [1, 8]]), ALU.subtract, ["R", "ccol"], ["biasT"])
            pcount = 0

            def build_qm(n_):
                h_ = n_ // 8; J_ = n_ % 8; kt_ = h_ // 2; pb_q = (h_ % 2) * 64; qb_ = n_ % 2; qk_ = "qm%d" % qb_
                MS('dve', qm[qb_][:], 0.0, [qk_])
                CP('dve', qm[qb_][pb_q:pb_q + 64, :], qTs[pb_q:pb_q + 64, kt_, J_ * 512:(J_ + 1) * 512], ["qTs", qk_], [qk_])
            for h in range(8):
                kt = h // 2; pb = (h % 2) * 64
                for J in range(8):
                    O, ok = psn(0, 2)
                    ni = 4 * J + 4
                    Ss = {}
                    qb = (h * 8 + J) % 2; qk = "qm%d" % qb
                    if h == 0 and J == 0:
                        build_qm(0)

                    def issue_S(i):
                        c0 = max(i - 4 * J, 0) * 128
                        S, sk = psn(2, 6)
                        MM(S[:, c0:512], kTs[:, kt, i * 128:(i + 1) * 128], qm[qb][:, c0:512], True, True, ["kTs", qk], [sk])
                        Ss[i] = (S, sk, c0)
                    issue_S(0)
                    issue_S(1)
                    if h * 8 + J + 1 < 64:
                        build_qm(h * 8 + J + 1)
                    for i in range(ni):
                        if i + 2 < ni:
                            issue_S(i + 2)
                        S, sk, c0 = Ss.pop(i)
                        pb_ = pcount % 4; pcount += 1; pkey = "Pt%d" % pb_
                        ACT(Pt[pb_][:, c0:512], S[:, c0:512], AF.Exp, [sk, "biasT"], [pkey], bias=biasT[:, i, J, h:h + 1], scale=0.125)
                        if i >= 4 * J:
                            TT('dve', Pt[pb_][:, c0:c0 + 128], Pt[pb_][:, c0:c0 + 128], trib[:], ALU.mult, [pkey, "trib"], [pkey])
                        MM(O[:, c0:512], vs[:, i, h * 128:(h + 1) * 128], Pt[pb_][:, c0:512], i == 0, i == ni - 1, ["vs", pkey], [ok], sig=(i == ni - 1))
                    P.op('dve', 'reciprocal', dict(out=rd[64:128, :], in_=O[64:128, :]), [ok], ["rd"])
                    TT('dve', fost[kt % 2][pb:pb + 64, J * 512:(J + 1) * 512], O[0:64, :], rd[64:128, :], ALU.mult, [ok, "rd"], ["fost%d" % (kt % 2)])
                if h % 2 == 1:
                    P.dma('sp', D["oT"][kt * 128:(kt + 1) * 128, :], fost[kt % 2][:], r=["fost%d" % (kt % 2)], w=["oT"], acc=True)
            P.barrier()

    if 4 in phases:
        with contextlib.ExitStack() as ph:
            f2 = lambda n, sh, dt=F32: sbt(ph, n, sh, dt)
            lre = f2("lre", [128, 16]); lim = f2("lim", [128, 16]); ldt = f2("ldt", [128, 16]); dtt = f2("dtt", [128, 16])
            are = f2("are", [128, 16]); th = f2("th", [128, 16]); mag = f2("mag", [128, 16])
            cs1 = f2("cs1", [128, 16]); sn1 = f2("sn1", [128, 16]); c512 = f2("c512", [128, 16]); s512 = f2("s512", [128, 16]); ns512 = f2("ns512", [128, 16])
            tmpa = f2("tmpa", [128, 16]); tmpb = f2("tmpb", [128, 16]); nr = f2("nr", [128, 16]); ni_ = f2("ni_", [128, 16])
            den = f2("den", [128, 16]); cr = f2("cr", [128, 16]); ci = f2("ci", [128, 16])
            braw = f2("braw", [128, 16, 16]); biraw = f2("biraw", [128, 16, 16]); Bre = f2("Bre", [128, 16, 16]); Bim = f2("Bim", [128, 16, 16]); btmp = f2("btmp", [128, 16, 16])
            BreT = f2("BreT", [128, 16, 128], BF16); BimT = f2("BimT", [128, 16, 128], BF16)
            CWr = f2("CWr", [128, 16, 128], BF16); NCr = f2("NCr", [128, 16, 128], BF16); NCi = f2("NCi", [128, 16, 128], BF16)
            cosA = f2("cosA", [128, 16, 512]); sinA = f2("sinA", [128, 16, 512])
            dsk = f2("dsk", [128, 4]); bgl = f2("bgl", [128, 4]); cmask = f2("cmask", [128, 128])
            Wg = f2("Wg", [128, 4, 512], BF16); wgk = load_w(Wg, "wb_glu", 512, 1)
            P.dma('sp', lre[:], _ap(D["lam_re"], 0, [[1, 128], [128, 16]]), w=["lre"])
            P.dma('sp', lim[:], _ap(D["lam_im"], 0, [[1, 128], [128, 16]]), w=["lim"])
            for gl in range(2):
                P.dma('sp', ldt[gl * 64:(gl + 1) * 64, :], _ap(D["log_dt"], gl, [[0, 64], [2, 16]]), w=["ldt"], acc=True)
            for half in range(2):
                P.dma('sp', braw[:, half * 8:(half + 1) * 8, :], _ap(D["b_re"], half * 8 * 2048, [[16, 128], [2048, 8], [1, 16]]), w=["braw"], acc=True)
                P.dma('sp', biraw[:, half * 8:(half + 1) * 8, :], _ap(D["b_im"], half * 8 * 2048, [[16, 128], [2048, 8], [1, 16]]), w=["biraw"], acc=True)
            P.dma('sp', dsk[:], D["d_skip"].rearrange("(k p) -> p k", p=128), w=["dsk"])
            P.dma('sp', bgl[:], D["b_glu"].rearrange("(k p) -> p k", p=128), w=["bgl"])
            P.dma('sp', cmask[:], D["c_cmask"], w=["cmask"])
            ACT(dtt[:], ldt[:], AF.Exp, ["ldt"], ["dtt"])
            TT('dve', are[:], lre[:], dtt[:], ALU.mult, ["lre", "dtt"], ["are"])
            TT('dve', th[:], lim[:], dtt[:], ALU.mult, ["lim", "dtt"], ["th"])
            ACT(mag[:], are[:], AF.Exp, ["are"], ["mag"])

            kac = f2("kac", [128, 16]); rr = f2("rr", [128, 16]); nrr = f2("nrr", [128, 16])
            TS('dve', kac[:], th[:], PI, None, ALU.is_ge, None, ["th"], ["kac"])
            for m_ in (3.0, 5.0, 7.0):
                TS('dve', tmpa[:], th[:], m_ * PI, None, ALU.is_ge, None, ["th"], ["tmpa"])
                TT('dve', kac[:], kac[:], tmpa[:], ALU.add, ["kac", "tmpa"], ["kac"])
            STT('dve', rr[:], kac[:], -2.0 * PI, th[:], ALU.mult, ALU.add, ["kac", "th"], ["rr"])
            ACT(sn1[:], rr[:], AF.Sin, ["rr"], ["sn1"])
            TS('dve', nrr[:], rr[:], -1.0, None, ALU.mult, None, ["rr"], ["nrr"])
            TT('dve', nrr[:], nrr[:], rr[:], ALU.max, ["nrr", "rr"], ["nrr"])
            TS('dve', nrr[:], nrr[:], -1.0, 0.5 * PI, ALU.mult, ALU.add, ["nrr"], ["nrr"])
            ACT(cs1[:], nrr[:], AF.Sin, ["nrr"], ["cs1"])
            TT('dve', nr[:], mag[:], cs1[:], ALU.mult, ["mag", "cs1"], ["nr"])
            TS('dve', nr[:], nr[:], -1.0, None, ALU.add, None, ["nr"], ["nr"])
            TT('dve', ni_[:], mag[:], sn1[:], ALU.mult, ["mag", "sn1"], ["ni_"])
            TT('dve', den[:], lre[:], lre[:], ALU.mult, ["lre"], ["den"])
            TT('dve', tmpb[:], lim[:], lim[:], ALU.mult, ["lim"], ["tmpb"])
            TT('dve', den[:], den[:], tmpb[:], ALU.add, ["den", "tmpb"], ["den"])
            P.op('dve', 'reciprocal', dict(out=den[:], in_=den[:]), ["den"], ["den"])
            TT('dve', cr[:], nr[:], lre[:], ALU.mult, ["nr", "lre"], ["cr"])
            TT('dve', tmpb[:], ni_[:], lim[:], ALU.mult, ["ni_", "lim"], ["tmpb"])
            TT('dve', cr[:], cr[:], tmpb[:], ALU.add, ["cr", "tmpb"], ["cr"])
            TT('dve', cr[:], cr[:], den[:], ALU.mult, ["cr", "den"], ["cr"])
            TT('dve', ci[:], ni_[:], lre[:], ALU.mult, ["ni_", "lre"], ["ci"])
            TT('dve', tmpb[:], nr[:], lim[:], ALU.mult, ["nr", "lim"], ["tmpb"])
            TT('dve', ci[:], ci[:], tmpb[:], ALU.subtract, ["ci", "tmpb"], ["ci"])
            TT('dve', ci[:], ci[:], den[:], ALU.mult, ["ci", "den"], ["ci"])
            bc = lambda t: _ap(t[:], 0, [list(t[:].ap[0]), [1, 16], [0, 16]])
            TT('dve', Bre[:], braw[:], bc(cr), ALU.mult, ["braw", "cr"], ["Bre"])
            TT('dve', btmp[:], biraw[:], bc(ci), ALU.mult, ["biraw", "ci"], ["btmp"])
            TT('dve', Bre[:], Bre[:], btmp[:], ALU.subtract, ["Bre", "btmp"], ["Bre"])
            TT('dve', Bim[:], biraw[:], bc(cr), ALU.mult, ["biraw", "cr"], ["Bim"])
            TT('dve', btmp[:], braw[:], bc(ci), ALU.mult, ["braw", "ci"], ["btmp"])
            TT('dve', Bim[:], Bim[:], btmp[:], ALU.add, ["Bim", "btmp"], ["Bim"])
            with contextlib.ExitStack() as ph2:
                Bw = [sbt(ph2, "Bw%d" % i, [128, 16, 128]) for i in range(2)]
                cnat = [sbt(ph2, "cnat%d" % i, [128, 4, 64]) for i in range(2)]
                Af = sbt(ph2, "Af", [128, 128])
                for i, (Bsrc, BT, bk) in enumerate(((Bre, BreT, "Bre"), (Bim, BimT, "Bim"))):
                    MS('dve', Bw[i][:], 0.0, ["Bw%d" % i])
                    for pr in range(16):
                        c0 = (pr % 4) * 32
                        CP('dve', Bw[i][0:64, pr, c0:c0 + 16], Bsrc[0:64, pr, :], [bk, "Bw%d" % i], ["Bw%d" % i])
                        CP('dve', Bw[i][64:128, pr, c0 + 16:c0 + 32], Bsrc[64:128, pr, :], [bk, "Bw%d" % i], ["Bw%d" % i])
                    for pr in range(16):
                        ps, pk = psn()
                        TR(ps[:, 0:128], Bw[i][:, pr, :], identf[:], ["Bw%d" % i, "identf"], [pk])
                        evac(BT[:, pr, :], ps[:, 0:128], [pk], ["BT%d" % i])
                for i, cname in enumerate(("c_re", "c_im")):
                    P.dma('sp', cnat[i][:], D[cname].rearrange("g h p -> (g h) p").rearrange("(j r) p -> r j p", r=128), w=["cnat%d" % i])
                for tbl in (CWr, NCr, NCi):
                    MS('dve', tbl[:], 0.0, ["CW"])
                for i in range(2):
                    for j in range(4):
                        CP('dve', Af[:, 0:64], cnat[i][:, j, :], ["cnat%d" % i, "Af"], ["Af"])
                        CP('dve', Af[:, 64:128], cnat[i][:, j, :], ["cnat%d" % i, "Af"], ["Af"])
                        TT('dve', Af[:], Af[:], cmask[:], ALU.mult, ["Af", "cmask"], ["Af"])
                        ps, pk = psn()
                        TR(ps[:, 0:128], Af[:], identf[:], ["Af", "identf"], [pk])
                        for pp in range(4):
                            pr = 4 * j + pp; c0 = pp * 32
                            if i == 0:
                                CP('dve', CWr[:, pr, c0:c0 + 32], ps[:, c0:c0 + 32], [pk, "CW"], ["CW"])
                                TS('dve', NCr[:, pr, c0:c0 + 32], ps[:, c0:c0 + 32], -1.0, None, ALU.mult, None, [pk, "CW"], ["CW"])
                            else:
                                TS('dve', NCi[:, pr, c0:c0 + 32], ps[:, c0:c0 + 32], -1.0, None, ALU.mult, None, [pk, "CW"], ["CW"])
                cE = sbt(ph2, "cE", [128, 16]); sE = sbt(ph2, "sE", [128, 16]); dtmp = sbt(ph2, "dtmp", [128, 16, 256])
                MS('dve', cosA[:, :, 0:1], 1.0, ["cosA"]); MS('dve', sinA[:, :, 0:1], 0.0, ["sinA"])
                CP('dve', cE[:], cs1[:], ["cs1"], ["cE"]); CP('dve', sE[:], sn1[:], ["sn1"], ["sE"])
                m_ = 1
                while m_ <= 256:
                    bE = lambda t: _ap(t[:], 0, [list(t[:].ap[0]), [1, 16], [0, m_]])
                    co_o = cosA[:, :, 0:m_]; si_o = sinA[:, :, 0:m_]; co_n = cosA[:, :, m_:2 * m_]; si_n = sinA[:, :, m_:2 * m_]; tm = dtmp[:, :, 0:m_]
                    TT('dve', co_n, co_o, bE(cE), ALU.mult, ["cosA", "cE"], ["cosA"])
                    TT('dve', tm, si_o, bE(sE), ALU.mult, ["sinA", "sE"], ["dtmp"])
                    TT('dve', co_n, co_n, tm, ALU.subtract, ["cosA", "dtmp"], ["cosA"])
                    TT('dve', si_n, si_o, bE(cE), ALU.mult, ["sinA", "cE"], ["sinA"])
                    TT('dve', tm, co_o, bE(sE), ALU.mult, ["cosA", "sE"], ["dtmp"])
                    TT('dve', si_n, si_n, tm, ALU.add, ["sinA", "dtmp"], ["sinA"])
                    TT('dve', tmpa[:], cE[:], sE[:], ALU.mult, ["cE", "sE"], ["tmpa"])
                    TT('dve', tmpb[:], sE[:], sE[:], ALU.mult, ["sE"], ["tmpb"])
                    TT('dve', cE[:], cE[:], cE[:], ALU.mult, ["cE"], ["cE"])
                    TT('dve', cE[:], cE[:], tmpb[:], ALU.subtract, ["cE", "tmpb"], ["cE"])
                    TS('dve', sE[:], tmpa[:], 2.0, None, ALU.mult, None, ["tmpa"], ["sE"])
                    m_ *= 2
                CP('dve', c512[:], cE[:], ["cE"], ["c512"]); CP('dve', s512[:], sE[:], ["sE"], ["s512"])
                TS('dve', ns512[:], s512[:], -1.0, None, ALU.mult, None, ["s512"], ["ns512"])

                P.barrier()
            NB = 2
            wt = {n: [f2("%s_%d" % (n, i), [128, 512]) for i in range(NB)] for n in ("t1", "t2", "t3", "t4", "zr", "zi")}
            pt_ = {n: [f2("%s_%d" % (n, i), [128, 512], BF16) for i in range(NB)] for n in ("p1", "p2", "p3", "p4")}
            ini = [[f2("ini%s%d" % (a, i), [128, 16]) for i in range(2)] for a in "ri"]
            ut = [f2("ut%d" % i, [128, 4, 512], BF16) for i in range(2)]
            ypre = f2("ypre", [128, 512]); ysq = f2("ysq", [128, 512]); ysg = f2("ysg", [128, 512])
            y2f = f2("y2f", [128, 4, 512]); y2b = f2("y2b", [128, 4, 512], BF16); s5st = [f2("s5st%d" % i, [128, 4, 512], BF16) for i in range(2)]
            gsg = f2("gsg", [128, 512])
            for a in range(2):
                MS('dve', ini[a][0][:], 0.0, ["ini%d_0" % a])
            ucount = 0
            P.dma('sp', ut[0][:], hTv("uT")[:, :, 0:512], r=["uT"], w=["ut0"])
            for c in range(8):
                cb = c % 2; cs = slice(c * 512, (c + 1) * 512)
                if c + 1 < 8:
                    P.dma('sp', ut[1 - cb][:], hTv("uT")[:, :, (c + 1) * 512:(c + 2) * 512], r=["uT"], w=["ut%d" % (1 - cb)])
                for T in range(4):
                    Y, yk = psn(0, 2)
                    for pi0 in (0, 2):
                        units = []
                        for pi in (pi0, pi0 + 1):
                            pr = 4 * T + pi
                            b = ucount % NB; ucount += 1
                            A, ak = psn(2, 6); Bp, bk = psn(2, 6)
                            units.append(dict(pi=pi, pr=pr, b=b, A=A, ak=ak, Bp=Bp, bk=bk, co=cosA[:, pr, :], si=sinA[:, pr, :]))
                        kf = lambda u, n: "%s_%d" % (n, u["b"])

                        def st_mm(u):
                            MM(u["A"][:, :], BreT[:, u["pr"], :], ut[cb][:, T, :], True, True, ["BT0", "ut%d" % cb], [u["ak"]])
                            MM(u["Bp"][:, :], BimT[:, u["pr"], :], ut[cb][:, T, :], True, True, ["BT1", "ut%d" % cb], [u["bk"]])

                        def st_t1(u):
                            TT('dve', wt["t1"][u["b"]][:], u["A"][:, :], u["co"], ALU.mult, [u["ak"], "cosA"], [kf(u, "t1")])

                        def st_t2(u):
                            TT('dve', wt["t2"][u["b"]][:], u["Bp"][:, :], u["si"], ALU.mult, [u["bk"], "sinA"], [kf(u, "t2")])

                        def st_a1(u):
                            TT('dve', wt["t1"][u["b"]][:], wt["t1"][u["b"]][:], wt["t2"][u["b"]][:], ALU.add, [kf(u, "t1"), kf(u, "t2")], [kf(u, "t1")])

                        def st_t3(u):
                            TT('dve', wt["t3"][u["b"]][:], u["Bp"][:, :], u["co"], ALU.mult, [u["bk"], "cosA"], [kf(u, "t3")])

                        def st_t4(u):
                            TT('dve', wt["t4"][u["b"]][:], u["A"][:, :], u["si"], ALU.mult, [u["ak"], "sinA"], [kf(u, "t4")])

                        def st_a3(u):
                            TT('dve', wt["t3"][u["b"]][:], wt["t3"][u["b"]][:], wt["t4"][u["b"]][:], ALU.subtract, [kf(u, "t3"), kf(u, "t4")], [kf(u, "t3")])

                        def st_sr(u):
                            mg = mag[:, u["pr"]:u["pr"] + 1].to_broadcast([128, 512])
                            P.op('dve', 'tensor_tensor_scan', dict(out=wt["zr"][u["b"]][:], data0=mg, data1=wt["t1"][u["b"]][:], initial=ini[0][cb][:, u["pr"]:u["pr"] + 1], op0=ALU.mult, op1=ALU.add),
                                 ["mag", kf(u, "t1"), "ini0_%d" % cb], [kf(u, "zr")])

                        def st_si(u):
                            mg = mag[:, u["pr"]:u["pr"] + 1].to_broadcast([128, 512])
                            P.op('dve', 'tensor_tensor_scan', dict(out=wt["zi"][u["b"]][:], data0=mg, data1=wt["t3"][u["b"]][:], initial=ini[1][cb][:, u["pr"]:u["pr"] + 1], op0=ALU.mult, op1=ALU.add),
                                 ["mag", kf(u, "t3"), "ini1_%d" % cb], [kf(u, "zi")])

                        def st_carry(u):
                            if c >= 7:
                                return
                            nb_ = 1 - cb; pr = u["pr"]; bb = u["b"]
                            zrl = wt["zr"][bb][:, 511:512]; zil = wt["zi"][bb][:, 511:512]
                            ta = tmpa[:, 2 * bb:2 * bb + 1]; tb_ = tmpa[:, 2 * bb + 1:2 * bb + 2]; tk = "tmpa%d" % bb
                            TS('dve', ta, zrl, c512[:, pr:pr + 1], None, ALU.mult, None, [kf(u, "zr"), "c512"], [tk])
                            STT('dve', ini[0][nb_][:, pr:pr + 1], zil, ns512[:, pr:pr + 1], ta, ALU.mult, ALU.add, [kf(u, "zi"), "ns512", tk], ["ini0_%d" % nb_])
                            TS('dve', tb_, zrl, s512[:, pr:pr + 1], None, ALU.mult, None, [kf(u, "zr"), "s512"], [tk])
                            STT('dve', ini[1][nb_][:, pr:pr + 1], zil, c512[:, pr:pr + 1], tb_, ALU.mult, ALU.add, [kf(u, "zi"), "c512", tk], ["ini1_%d" % nb_])

                        def st_pool(u):
                            bb = u["b"]
                            TT('pool', pt_["p1"][bb][:], wt["zr"][bb][:], u["co"], ALU.mult, [kf(u, "zr"), "cosA"], [kf(u, "p1")])
                            TT('pool', pt_["p2"][bb][:], wt["zi"][bb][:], u["si"], ALU.mult, [kf(u, "zi"), "sinA"], [kf(u, "p2")])
                            TT('pool', pt_["p3"][bb][:], wt["zr"][bb][:], u["si"], ALU.mult, [kf(u, "zr"), "sinA"], [kf(u, "p3")])
                            TT('pool', pt_["p4"][bb][:], wt["zi"][bb][:], u["co"], ALU.mult, [kf(u, "zi"), "cosA"], [kf(u, "p4")])

                        def st_y(u):
                            bb = u["b"]; pi = u["pi"]; pr = u["pr"]
                            MM(Y[:, :], CWr[:, pr, :], pt_["p1"][bb][:], pi == 0, False, ["CW", kf(u, "p1")], [yk], sig=False)
                            MM(Y[:, :], NCr[:, pr, :], pt_["p2"][bb][:], False, False, ["CW", kf(u, "p2")], [yk], sig=False)
                            MM(Y[:, :], NCi[:, pr, :], pt_["p3"][bb][:], False, False, ["CW", kf(u, "p3")], [yk], sig=False)
                            MM(Y[:, :], NCi[:, pr, :], pt_["p4"][bb][:], False, pi == 3, ["CW", kf(u, "p4")], [yk], sig=True)
                        for stage in (st_mm, st_t1, st_t2, st_a1, st_t3, st_t4, st_a3, st_sr, st_si, st_carry, st_pool, st_y):
                            for u in units:
                                stage(u)
                    STT('dve', ypre[:], ut[cb][:, T, :], dsk[:, T:T + 1], Y[:, :], ALU.mult, ALU.add, ["ut%d" % cb, "dsk", yk], ["ypre"])
                    ACT(ysq[:], ypre[:], AF.Square, ["ypre"], ["ysq"])
                    TS('dve', ysq[:], ysq[:], 0.044715, 1.0, ALU.mult, ALU.add, ["ysq"], ["ysq"])
                    TT('dve', ysq[:], ysq[:], ypre[:], ALU.mult, ["ysq", "ypre"], ["ysq"])
                    ACT(ysg[:], ysq[:], AF.Sigmoid, ["ysq"], ["ysg"], scale=1.5957691216057308)
                    TT('dve', y2f[:, T, :], ypre[:], ysg[:], ALU.mult, ["ypre", "ysg"], ["y2f%d" % T])
                    CP('dve', y2b[:, T, :], y2f[:, T, :], ["y2f%d" % T], ["y2b"])
                sb_ = c % 2
                for ft in range(4):
                    G, gk = psn(2, 6)
                    for k in range(4):
                        MM(G[:, :], Wg[:, k, ft * 128:(ft + 1) * 128], y2b[:, k, :], k == 0, k == 3, [wgk, "y2b"], [gk], sig=(k == 3))
                    ACT(gsg[:], G[:, :], AF.Sigmoid, [gk, "bgl"], ["gsg"], bias=bgl[:, ft:ft + 1], scale=1.0)
                    TT('dve', s5st[sb_][:, ft, :], y2f[:, ft, :], gsg[:], ALU.mult, ["y2f%d" % ft, "gsg"], ["s5st%d" % sb_])
                P.dma('sp', hTv("s5T")[:, :, cs], s5st[sb_][:], r=["s5st%d" % sb_], w=["s5T"], acc=True)
            P.barrier()

    def outproj(srcA, srcB, wname, xin, xout, hTout, gname, goff):
        with contextlib.ExitStack() as ph:
            N_ = mk_norm(ph, gname, goff)
            W = sbt(ph, "Wo", [128, 8, 1024], BF16); wk = load_w(W, wname, 1024)
            cat = [sbt(ph, "cat%d" % i, [128, 8, 512], BF16) for i in range(2)]
            xt = [sbt(ph, "oxt%d" % i, [128, 1024]) for i in range(2)]
            hTs = [sbt(ph, "ohTs%d" % i, [128, 8, 512], BF16) for i in range(2)]
            def ld_cat(blk_):
                b_ = blk_ % 2; cs_ = slice(blk_ * 512, (blk_ + 1) * 512)
                P.dma('sp', cat[b_][:, 0:4, :], hTv(srcA)[:, :, cs_], r=[srcA], w=["cat%d" % b_], acc=True)
                P.dma('sp', cat[b_][:, 4:8, :], hTv(srcB)[:, :, cs_], r=[srcB], w=["cat%d" % b_], acc=True)

            def ld_x(tb_):
                P.dma('sp', xt[tb_ % 2][:], D[xin][tb_ * 128:(tb_ + 1) * 128, :], r=[xin], w=["oxt%d" % (tb_ % 2)])
            ld_cat(0); ld_x(0)
            pend = None

            def flush(pd):
                nb__, b__, tt__, blk__ = pd
                norm_B(N_, nb__, hTs[b__], "ohTs%d" % b__, tt__)
                if tt__ == 3:
                    P.dma('sp', hTv(hTout)[:, :, blk__ * 512:(blk__ + 1) * 512], hTs[b__][:], r=["ohTs%d" % b__], w=[hTout], acc=True)
            for blk in range(8):
                b = blk % 2; cs = slice(blk * 512, (blk + 1) * 512); ck = "cat%d" % b
                if blk + 1 < 8:
                    ld_cat(blk + 1)
                for tt in range(4):
                    tb = blk * 4 + tt; xb = tb % 2; xk = "oxt%d" % xb
                    if tb + 1 < 32:
                        ld_x(tb + 1)
                    ps0, k0 = psn(); ps1, k1 = psn()
                    for k in range(8):
                        lt = cat[b][:, k, tt * 128:(tt + 1) * 128]
                        MM(ps0[:, :], lt, W[:, k, 0:512], k == 0, k == 7, [wk, ck], [k0], sig=False)
                        MM(ps1[:, :], lt, W[:, k, 512:1024], k == 0, k == 7, [wk, ck], [k1], sig=(k == 7))
                    TT('dve', xt[xb][:, 0:512], xt[xb][:, 0:512], ps0[:, :], ALU.add, [xk, k0], [xk])
                    TT('dve', xt[xb][:, 512:1024], xt[xb][:, 512:1024], ps1[:, :], ALU.add, [xk, k1], [xk])
                    P.dma('sp', D[xout][tb * 128:(tb + 1) * 128, :], xt[xb][:], r=[xk], w=[xout], acc=True)
                    nb_ = norm_A(N_, xt[xb][:], xk)
                    if pend is not None:
                        flush(pend)
                    pend = (nb_, b, tt, blk)
            flush(pend)
            P.barrier()

    def prefetch_mlp(stk, wup, wdn):
        Wu = sbt(stk, "Wu", [128, 8, 4096], BF16); wuk = load_w(Wu, wup, 1024, 4)
        return (Wu, wuk)

    big0 = contextlib.ExitStack()
    pre0 = prefetch_mlp(big0, "wb_up0", "wb_dn0") if (5 in phases and 6 in phases) else None
    if 5 in phases:
        outproj("oT", "s5T", "wb_out0", "x", "xr1", "hT1", "g_mlp", 0)

    def mlp(wup, wdn, hTin, xin, xout, hTout, final=False, gname=None, goff=0, pre=None):
        with contextlib.ExitStack() as ph:
            N_ = mk_norm(ph, None if final else gname, goff)
            if pre is None:
                Wu = sbt(ph, "Wu", [128, 8, 4096], BF16); wuk = load_w(Wu, wup, 1024, 4)
                Wd = sbt(ph, "Wd", [128, 32, 1024], BF16); wdk = load_w(Wd, wdn, 4096, 4)
            else:
                Wu, wuk = pre
                Wd = sbt(ph, "Wd", [128, 32, 1024], BF16); wdk = load_w(Wd, wdn, 4096, 4)
            hb = [sbt(ph, "mhb%d" % i, [128, 8, 256], BF16) for i in range(2)]
            aT = sbt(ph, "aT", [128, 32, 256], BF16)
            rl = [sbt(ph, "rl%d" % i, [128, 256]) for i in range(2)]
            xt = [sbt(ph, "mxt%d" % i, [128, 1024]) for i in range(2)]
            if final:
                gfin = sbt(ph, "gfin", [128, 1024])
                P.dma('sp', gfin[:], _ap(D["g_final"], 0, [[0, 128], [1, 1024]]), w=["gfin"])
            else:
                hTs = [sbt(ph, "mhTs%d" % i, [128, 8, 256], BF16) for i in range(2)]
            def ld_x(tb_):
                P.dma('sp', xt[tb_ % 2][:], D[xin][tb_ * 128:(tb_ + 1) * 128, :], r=[xin], w=["mxt%d" % (tb_ % 2)])
            P.dma('sp', hb[0][:], hTv(hTin)[:, :, 0:256], r=[hTin], w=["mhb0"])
            ld_x(0)
            pend = [None]

            def mflush(pd):
                nb__, hbuf__, tb__ = pd
                norm_B(N_, nb__, hTs[hbuf__], "mhTs%d" % hbuf__, tb__ % 2)
                if tb__ % 2 == 1:
                    g2 = tb__ // 2
                    P.dma('sp', hTv(hTout)[:, :, g2 * 256:(g2 + 1) * 256], hTs[hbuf__][:], r=["mhTs%d" % hbuf__], w=[hTout], acc=True)
            for blk in range(16):
                b = blk % 2; cs = slice(blk * 256, (blk + 1) * 256); hk = "mhb%d" % b
                if blk + 1 < 16:
                    P.dma('sp', hb[1 - b][:], hTv(hTin)[:, :, (blk + 1) * 256:(blk + 2) * 256], r=[hTin], w=["mhb%d" % (1 - b)])
                for ft in range(32):
                    ps, pk = psn()
                    for k in range(8):
                        MM(ps[:, 0:256], Wu[:, k, ft * 128:(ft + 1) * 128], hb[b][:, k, :], k == 0, k == 7, [wuk, hk], [pk], sig=(k == 7))
                    rb = ft % 2
                    ACT(rl[rb][:], ps[:, 0:256], AF.Relu, [pk], ["rl%d" % rb])
                    TT('dve' if ft % 2 == 0 else 'pool', aT[:, ft, :], rl[rb][:], rl[rb][:], ALU.mult, ["rl%d" % rb], ["aT%d" % ft])
                akeys = ["aT%d" % ft for ft in range(32)]
                for tt in range(2):
                    tb = blk * 2 + tt; xb = tb % 2; xk = "mxt%d" % xb
                    if tb + 1 < 32:
                        ld_x(tb + 1)
                    ps0, k0 = psn(); ps1, k1 = psn()
                    for k in range(32):
                        lt = aT[:, k, tt * 128:(tt + 1) * 128]
                        MM(ps0[:, :], lt, Wd[:, k, 0:512], k == 0, k == 31, [wdk, "aT%d" % k], [k0], sig=False)
                        MM(ps1[:, :], lt, Wd[:, k, 512:1024], k == 0, k == 31, [wdk, "aT%d" % k], [k1], sig=(k == 31))
                    TT('dve', xt[xb][:, 0:512], xt[xb][:, 0:512], ps0[:, :], ALU.add, [xk, k0], [xk])
                    TT('dve', xt[xb][:, 512:1024], xt[xb][:, 512:1024], ps1[:, :], ALU.add, [xk, k1], [xk])
                    if not final:
                        hbuf = (tb // 2) % 2
                        P.dma('sp', D[xout][tb * 128:(tb + 1) * 128, :], xt[xb][:], r=[xk], w=[xout], acc=True)
                        nb_ = norm_A(N_, xt[xb][:], xk)
                        if pend[0] is not None:
                            mflush(pend[0])
                        pend[0] = (nb_, hbuf, tb)
                    else:
                        j = norm_stats(N_, xt[xb][:], xk)
                        TS('dve', xt[xb][:], xt[xb][:], N_["rs"][:, j:j + 1], None, ALU.mult, None, [xk, "nrs%d" % j], [xk])
                        TT('dve', xt[xb][:], xt[xb][:], gfin[:], ALU.mult, [xk, "gfin"], [xk])
                        P.dma('sp', D["out"][tb * 128:(tb + 1) * 128, :], xt[xb][:], r=[xk], w=["out"], acc=True)
            if not final and pend[0] is not None:
                mflush(pend[0])
            P.barrier()

    if 6 in phases:
        mlp("wb_up0", "wb_dn0", "hT1", "xr1", "xr2", "hT2", gname="g_odd", pre=pre0)
    big0.close()

    if 7 in phases:
        with contextlib.ExitStack() as ph:
            W = sbt(ph, "W7", [128, 8, 3584], BF16); wk = load_w(W, "wb_in1", 1024, 4)
            hb = [sbt(ph, "h7b%d" % i, [128, 8, 512], BF16) for i in range(2)]
            cw = sbt(ph, "cw", [128, 4, 3]); rc = sbt(ph, "rc", [128, 32, 64]); rsn = sbt(ph, "rsn", [128, 32, 64])
            qdec = sbt(ph, "qdec", [128, 4]); kdec = sbt(ph, "kdec", [128, 4])
            zb = [sbt(ph, "zb%d" % i, [128, 4, 514]) for i in range(2)]
            hcs = [sbt(ph, "hcs%d" % i, [128, 512]) for i in range(2)]
            acc = [sbt(ph, "cacc%d" % i, [128, 512]) for i in range(2)]
            cvst = [sbt(ph, "cvst%d" % i, [128, 4, 512], BF16) for i in range(2)]
            ra = [sbt(ph, "ra%d" % i, [128, 4, 64]) for i in range(2)]; rb = [sbt(ph, "rb%d" % i, [128, 4, 64]) for i in range(2)]
            qrot = [sbt(ph, "qrot%d" % i, [128, 4, 128]) for i in range(2)]
            sts = {n: [sbt(ph, "%sst%d" % (n, i), [128, 4, 512], BF16) for i in range(2)] for n in ("qd", "kd", "vt", "sg")}
            for j in range(3):
                P.dma('sp', cw[:, :, j], D["conv_w"][j].rearrange("(k p) -> p k", p=128), w=["cw"], acc=True)
            P.dma('sp', rc[:], D["c_rcos"].rearrange("(t p) i -> p t i", p=128), w=["rc"])
            P.dma('sp', rsn[:], D["c_rsin"].rearrange("(t p) i -> p t i", p=128), w=["rsn"])
            P.dma('sp', qdec[:], D["c_qdec"], w=["qdec"]); P.dma('sp', kdec[:], D["c_kdec"], w=["kdec"])
            MS('dve', zb[0][:, :, 0:2], 0.0, ["zb0h"])
            rcount = 0
            P.dma('sp', hb[0][:], hTv("hT2")[:, :, 0:512], r=["hT2"], w=["h7b0"])
            for blk in range(8):
                b = blk % 2; cs = slice(blk * 512, (blk + 1) * 512); hk = "h7b%d" % b
                if blk + 1 < 8:
                    P.dma('sp', hb[1 - b][:], hTv("hT2")[:, :, (blk + 1) * 512:(blk + 2) * 512], r=["hT2"], w=["h7b%d" % (1 - b)])

                def fm(col0):
                    ps, pk = psn()
                    for k in range(8):
                        MM(ps[:, :], W[:, k, col0:col0 + 128], hb[b][:, k, :], k == 0, k == 7, [wk, hk], [pk], sig=(k == 7))
                    return ps, pk
                for ft in range(4):
                    hbf = ft % 2
                    ps_h, kh = fm(ft * 128)
                    P.op('act', 'copy', dict(out=hcs[hbf][:], in_=ps_h[:, :]), [kh], ["hcs%d" % hbf])
                    ps_c, kc = fm(1024 + ft * 128)
                    zk = "zb%d_%d" % (b, ft)
                    TT('dve', zb[b][:, ft, 2:514], ps_c[:, :], hcs[hbf][:], ALU.mult, [kc, "hcs%d" % hbf], [zk])
                    hkey = "zb%dh" % b
                    TS('dve', acc[hbf][:], zb[b][:, ft, 2:514], cw[:, ft, 2:3], None, ALU.mult, None, [zk, "cw"], ["cacc%d" % hbf])
                    STT('dve', acc[hbf][:], zb[b][:, ft, 1:513], cw[:, ft, 1:2], acc[hbf][:], ALU.mult, ALU.add, [zk, hkey, "cw", "cacc%d" % hbf], ["cacc%d" % hbf])
                    STT('dve', acc[hbf][:], zb[b][:, ft, 0:512], cw[:, ft, 0:1], acc[hbf][:], ALU.mult, ALU.add, [zk, hkey, "cw", "cacc%d" % hbf], ["cacc%d" % hbf])
                    ps_b, kb = fm(512 + ft * 128)
                    TT('dve', cvst[b][:, ft, :], ps_b[:, :], acc[hbf][:], ALU.mult, [kb, "cacc%d" % hbf], ["cvst%d" % b])
                    CP('dve', zb[1 - b][:, ft, 0:2], zb[b][:, ft, 512:514], [zk, "zb%dh" % (1 - b)], ["zb%dh" % (1 - b)])
                P.dma('sp', hTv("convT")[:, :, cs], cvst[b][:], r=["cvst%d" % b], w=["convT"], acc=True)
                for tt in range(4):
                    tb = blk * 4 + tt

                    def tm(col0):
                        ps, pk = psn()
                        for k in range(8):
                            MM(ps[:, :], hb[b][:, k, tt * 128:(tt + 1) * 128], W[:, k, col0:col0 + 512], k == 0, k == 7, [wk, hk], [pk], sig=(k == 7))
                        return ps, pk
                    cosb = _ap(rc[:], tb * 64, [list(rc[:].ap[0]), [0, 4], [1, 64]])
                    sinb = _ap(rsn[:], tb * 64, [list(rsn[:].ap[0]), [0, 4], [1, 64]])
                    for nm, col0, dec in (("qd", 1536, qdec), ("kd", 2048, kdec)):
                        ps, pk = tm(col0)
                        r_ = rcount % 2; rcount += 1
                        pv = ps[:, :].rearrange("p (h d) -> p h d", h=4)
                        x1 = pv[:, :, 0:64]; x2 = pv[:, :, 64:128]
                        ka, kb_, kq = "ra%d" % r_, "rb%d" % r_, "qrot%d" % r_
                        TT('dve', ra[r_][:], x1, cosb, ALU.mult, [pk, "rc"], [ka])
                        TT('dve', rb[r_][:], x2, sinb, ALU.mult, [pk, "rsn"], [kb_])
                        TT('dve', qrot[r_][:, :, 0:64], ra[r_][:], rb[r_][:], ALU.subtract, [ka, kb_], [kq])
                        TT('dve', ra[r_][:], x1, sinb, ALU.mult, [pk, "rsn", kq], [ka])
                        TT('dve', rb[r_][:], x2, cosb, ALU.mult, [pk, "rc", kq], [kb_])
                        TT('dve', qrot[r_][:, :, 64:128], ra[r_][:], rb[r_][:], ALU.add, [ka, kb_], [kq])
                        decb = _ap(dec[:], 0, [list(dec[:].ap[0]), [1, 4], [0, 128]])
                        TT('dve', sts[nm][b][:, tt, :].rearrange("p (h d) -> p h d", h=4), qrot[r_][:], decb, ALU.mult, [kq, "qdec", "kdec"], ["%sst%d" % (nm, b)])
                    ps, pk = tm(2560)
                    evac(sts["vt"][b][:, tt, :], ps[:, :], [pk], ["vtst%d" % b])
                    ps, pk = tm(3072)
                    ACT(sts["sg"][b][:, tt, :], ps[:, :], AF.Silu, [pk], ["sgst%d" % b])
                for nm in ("qd", "kd", "vt", "sg"):
                    P.dma('sp', D[nm][cs, :].rearrange("(t p) c -> p t c", p=128), sts[nm][b][:], r=["%sst%d" % (nm, b)], w=[nm], acc=True)
            P.barrier()

    if 8 in phases:
        with contextlib.ExitStack() as ph:
            rmask = sbt(ph, "rmask", [128, 512]); stf = sbt(ph, "stf", [128, 4, 128]); stb = sbt(ph, "stb", [128, 4, 128], BF16)
            ld = {n: [sbt(ph, "%sl%d" % (n, i), [128, 4, 512], BF16) for i in range(2)] for n in ("qd", "kd", "vt", "sg")}
            qTs_ = [sbt(ph, "rqT%d" % i, [128, 512], BF16) for i in range(2)]; kTs_ = [sbt(ph, "rkT%d" % i, [128, 512], BF16) for i in range(2)]
            Pm = [sbt(ph, "Pm%d" % i, [128, 512], BF16) for i in range(2)]; on = [sbt(ph, "on%d" % i, [128, 512]) for i in range(2)]
            rg = [sbt(ph, "rg%d" % i, [128, 512], BF16) for i in range(2)]
            bs = sbt(ph, "bs", [128, 4, 6]); mv = sbt(ph, "mv", [128, 4, 2]); rstd = sbt(ph, "rstd", [128, 4]); sdv = sbt(ph, "sdv", [128, 4])
            retst = [sbt(ph, "retst%d" % i, [128, 4, 512], BF16) for i in range(2)]
            gam128 = [float((1.0 - 2.0 ** (-5.0 - h)) ** 128.0) for h in range(4)]
            P.dma('sp', rmask[:], D["c_rmask"], w=["rmask"])
            MS('dve', stf[:], 0.0, ["stf"]); MS('dve', stb[:], 0.0, ["stb"])
            def ld_grp(g_):
                for nm in ("qd", "kd", "vt", "sg"):
                    P.dma('sp', ld[nm][g_ % 2][:], D[nm][g_ * 512:(g_ + 1) * 512, :].rearrange("(t p) c -> p t c", p=128), r=[nm], w=["%sl%d" % (nm, g_ % 2)])
            ld_grp(0)
            for c in range(32):
                g4 = c // 4; b4 = g4 % 2; ci = c % 4; b = c % 2
                if ci == 0 and g4 + 1 < 8:
                    ld_grp(g4 + 1)
                lk = {nm: "%sl%d" % (nm, b4) for nm in ld}
                qd_c = ld["qd"][b4][:, ci, :]; kd_c = ld["kd"][b4][:, ci, :]; vt_c = ld["vt"][b4][:, ci, :]; sg_c = ld["sg"][b4][:, ci, :]
                hs = lambda h: slice(h * 128, (h + 1) * 128)
                pq, pqk = psbn()
                for h in range(4):
                    TR(pq[:, hs(h)], qd_c[:, hs(h)], identb[:], [lk["qd"], "identb"], [pqk], sig=(h == 3))
                P.op('act', 'copy', dict(out=qTs_[b][:], in_=pq[:, 0:512]), [pqk], ["rqT%d" % b])
                pk2, pkk = psbn()
                for h in range(4):
                    TR(pk2[:, hs(h)], kd_c[:, hs(h)], identb[:], [lk["kd"], "identb"], [pkk], sig=(h == 3))
                CP('dve', kTs_[b][:], pk2[:, 0:512], [pkk], ["rkT%d" % b])
                S, sk = psn()
                for h in range(4):
                    MM(S[:, hs(h)], kTs_[b][:, hs(h)], qTs_[b][:, hs(h)], True, True, ["rkT%d" % b, "rqT%d" % b], [sk], sig=(h == 3))
                TT('dve', Pm[b][:], S[:, :], rmask[:], ALU.mult, [sk, "rmask"], ["Pm%d" % b])
                O, ok = psn()
                for h in range(4):
                    MM(O[:, hs(h)], Pm[b][:, hs(h)], vt_c[:, hs(h)], True, False, ["Pm%d" % b, lk["vt"]], [ok], sig=False)
                    MM(O[:, hs(h)], qTs_[b][:, hs(h)], stb[:, h, :], False, True, ["rqT%d" % b, "stb"], [ok], sig=(h == 3))
                U, uk = psn()
                for h in range(4):
                    MM(U[:, hs(h)], kd_c[:, hs(h)], vt_c[:, hs(h)], True, True, [lk["kd"], lk["vt"]], [uk], sig=(h == 3))
                for h in range(4):
                    STT('dve', stf[:, h, :], stf[:, h, :], gam128[h], U[:, hs(h)], ALU.mult, ALU.add, ["stf", uk], ["stf"])
                CP('dve', stb[:], stf[:], ["stf"], ["stb"])
                for h in range(4):
                    P.op('dve', 'bn_stats', dict(out=bs[:, h, :], in_=O[:, hs(h)]), [ok], ["bs"])
                    P.op('dve', 'bn_aggr', dict(out=mv[:, h, :], in_=bs[:, h, :]), ["bs"], ["mv"])
                ACT(sdv[:], mv[:, :, 1], AF.Sqrt, ["mv"], ["sdv"], bias=EPS, scale=1.0)
                P.op('dve', 'reciprocal', dict(out=rstd[:], in_=sdv[:]), ["sdv"], ["rstd"])
                for h in range(4):
                    TS('dve', on[b][:, hs(h)], O[:, hs(h)], mv[:, h, 0:1], rstd[:, h:h + 1], ALU.subtract, ALU.mult, [ok, "mv", "rstd"], ["on%d" % b])
                TT('dve', rg[b][:], on[b][:], sg_c, ALU.mult, ["on%d" % b, lk["sg"]], ["rg%d" % b])
                pr_, prk = psbn()
                for h in range(4):
                    TR(pr_[:, hs(h)], rg[b][:, hs(h)], identb[:], ["rg%d" % b, "identb"], [prk], sig=(h == 3))
                P.op('act', 'copy', dict(out=retst[b4][:, :, ci * 128:(ci + 1) * 128], in_=pr_[:, 0:512].rearrange("p (h t) -> p h t", h=4)), [prk], ["retst%d" % b4])
                if ci == 3:
                    P.dma('sp', hTv("retT")[:, :, g4 * 512:(g4 + 1) * 512], retst[b4][:], r=["retst%d" % b4], w=["retT"], acc=True)
            P.barrier()

    big1 = contextlib.ExitStack()
    pre1 = prefetch_mlp(big1, "wb_up1", "wb_dn1") if (9 in phases and 10 in phases) else None
    if 9 in phases:
        outproj("convT", "retT", "wb_out1", "xr2", "xr3", "hT3", "g_mlp", 1024)
    if 10 in phases:
        mlp("wb_up1", "wb_dn1", "hT3", "xr3", None, None, final=True, pre=pre1)
    big1.close()

    P.barrier()
    P.emit()
    st.close()
    return nc


def _consts():
    bf = ml_dtypes.bfloat16
    c = {}
    c["c_identb"] = np.eye(128, dtype=np.float32).astype(bf)
    c["c_identf"] = np.eye(128, dtype=np.float32)
    s = np.arange(128)
    c["c_trib"] = (s[None, :] >= s[:, None]).astype(np.float32).astype(bf)
    inv = (1.0 / (np.float32(10000.0) ** np.linspace(0.0, 1.0, 64, dtype=np.float32))).astype(np.float32)
    ang = (np.arange(NT, dtype=np.float32)[:, None] * inv[None, :]).astype(np.float32)
    c["c_rcos"] = np.cos(ang).astype(np.float32)
    c["c_rsin"] = np.sin(ang).astype(np.float32)
    gam = (1.0 - 2.0 ** (-5.0 - np.arange(4, dtype=np.float64)))
    idx = np.arange(128, dtype=np.float64)
    c["c_qdec"] = (gam[None, :] ** (idx[:, None] + 1.0)).astype(np.float32)
    c["c_kdec"] = ((gam[None, :] ** (127.0 - idx[:, None])) * (128.0 ** -0.5)).astype(np.float32)
    tri = (s[None, :] >= s[:, None]).astype(np.float64)
    c["c_rmask"] = np.concatenate([tri * gam[h] ** (-128.0) for h in range(4)], axis=1).astype(np.float32)
    c["c_iota"] = np.tile(np.arange(512, dtype=np.float32)[None, :], (128, 1))
    m = np.arange(128)
    c["c_cmask"] = ((m[None, :] // 64) == ((m[:, None] // 16) % 2)).astype(np.float32)
    return c


def _perm_w_in1(w):
    idx = list(range(1536))
    for base in (1536, 2048):
        for h in range(4):
            idx += [base + h * 128 + 2 * i for i in range(64)] + [base + h * 128 + 2 * i + 1 for i in range(64)]
    idx += list(range(2560, 3584))
    return np.ascontiguousarray(w[:, idx])


def _host_inputs(inp):
    f = lambda a: np.ascontiguousarray(np.asarray(a, dtype=np.float32))
    sh = {
        "g_even": f(inp["even_norm_mix"])[0], "w_in0": f(inp["even_w_in"])[0], "b_forget": f(inp["even_b_forget"])[0],
        "log_dt": f(inp["even_s5_log_dt"])[0], "lam_re": f(inp["even_s5_lambda_re"])[0], "lam_im": f(inp["even_s5_lambda_im"])[0],
        "b_re": f(inp["even_s5_b_re"])[0], "b_im": f(inp["even_s5_b_im"])[0], "c_re": f(inp["even_s5_c_re"])[0], "c_im": f(inp["even_s5_c_im"])[0],
        "d_skip": f(inp["even_s5_d"])[0], "w_glu": f(inp["even_s5_w_glu"])[0], "b_glu": f(inp["even_s5_b_glu"])[0], "w_out0": f(inp["even_w_out"])[0],
        "g_odd": f(inp["odd_norm_mix"])[0], "w_in1": _perm_w_in1(f(inp["odd_w_in"])[0]), "conv_w": f(inp["odd_conv_w"])[0], "w_out1": f(inp["odd_w_out"])[0],
        "g_mlp": f(inp["mlp_norm"]), "w_up": f(inp["mlp_w_up"]), "w_down": f(inp["mlp_w_down"]), "g_final": f(inp["final_norm"]),
    }
    sh.update(_consts())
    return sh


_NC_CACHE = {}
FUSED = True
_GROUPS = [((1, 2, 3), (), ("oT", "uT")), ((4,), ("uT",), ("s5T",)), ((5, 6), ("oT", "s5T"), ("xr2", "hT2")),
           ((7, 8), ("hT2",), ("convT", "retT")), ((9, 10), ("convT", "retT", "xr2"), ())]


def kernel(**inputs):
    x = np.ascontiguousarray(np.asarray(inputs["x"], dtype=np.float32))
    shared = _host_inputs(inputs)
    if FUSED:
        if "nc" not in _NC_CACHE:
            _NC_CACHE["nc"] = build()
        nc = _NC_CACHE["nc"]
        in_maps = [dict(shared, x=x[b]) for b in range(8)]
        res = run_bass_kernel_spmd(nc, in_maps, core_ids=list(range(8)))
        return np.stack([np.asarray(r["out"], dtype=np.float32) for r in res.results], axis=0)
    carry = [dict() for _ in range(8)]
    res = None
    for gi, (phs, feed, outs) in enumerate(_GROUPS):
        key = "g%d" % gi
        if key not in _NC_CACHE:
            _NC_CACHE[key] = build(phases=phs, dbg=outs, feed=feed)
        nc = _NC_CACHE[key]
        in_maps = [dict(shared, x=x[b], **{n: carry[b][n] for n in feed}) for b in range(8)]
        res = run_bass_kernel_spmd(nc, in_maps, core_ids=list(range(8)))
        for b in range(8):
            for n in outs:
                carry[b][n] = np.ascontiguousarray(np.asarray(res.results[b][n]))
    return np.stack([np.asarray(r["out"], dtype=np.float32) for r in res.results], axis=0)
```
